# Optimizing a Trainium2 kernel written in Bass

```python
import math
import jax, jax.numpy as jnp
from jax import lax
import numpy as np

D_MODEL = 1024
BATCH = 2
SEQ = 8192
DEPTH = 1

CHUNK = 64
N_META = 16
PAD = CHUNK - N_META
NORM_EPS = 1e-6

SSD_EXPAND = 2
SSD_D_INNER = SSD_EXPAND * D_MODEL
SSD_HEAD_DIM = 64
SSD_HEADS = SSD_D_INNER // SSD_HEAD_DIM
SSD_GROUPS = 4
SSD_STATE = 128
SSD_CONV = 4
SSD_CONV_DIM = SSD_D_INNER + 2 * SSD_GROUPS * SSD_STATE

DN_QK_HEADS = 8
DN_V_HEADS = 16
DN_HEAD_K = 128
DN_HEAD_V = 128
DN_KEY_DIM = DN_QK_HEADS * DN_HEAD_K
DN_VALUE_DIM = DN_V_HEADS * DN_HEAD_V
DN_CONV = 4
DN_CONV_DIM = 2 * DN_KEY_DIM + DN_VALUE_DIM

N_BRANCH = 2
BRANCH_WIDTH = SSD_D_INNER

D_FF = -(-8 * D_MODEL // (3 * 256)) * 256

IN_SPLITS = (SSD_D_INNER, SSD_CONV_DIM, SSD_HEADS, DN_CONV_DIM, DN_V_HEADS, DN_V_HEADS,
             DN_VALUE_DIM, N_BRANCH * D_MODEL)
D_IN_PROJ = sum(IN_SPLITS)

kernel_name = "hybrid_ssd_gated_deltanet_block"


def rmsnorm(x, w):
    xf = x.astype(jnp.float32)
    y = xf * lax.rsqrt(jnp.mean(xf * xf, axis=-1, keepdims=True) + NORM_EPS)
    return (y * w.astype(jnp.float32)).astype(x.dtype)


def group_rms(y, groups):
    shp = y.shape
    yg = y.reshape(*shp[:-1], groups, shp[-1] // groups)
    yg = yg * lax.rsqrt(jnp.mean(yg * yg, axis=-1, keepdims=True) + NORM_EPS)
    return yg.reshape(shp)


def l2norm(x):
    return x * lax.rsqrt(jnp.sum(x * x, axis=-1, keepdims=True) + NORM_EPS)


def causal_depthwise_conv(x, w):
    k, c = w.shape
    return lax.conv_general_dilated(
        x, w[:, None, :].astype(x.dtype), window_strides=(1,), padding=[(k - 1, 0)],
        dimension_numbers=("NWC", "WIO", "NWC"), feature_group_count=c)


def segsum_exp(a_cum):
    l = a_cum.shape[-1]
    mask = jnp.tril(jnp.ones((l, l), dtype=bool))
    diff = a_cum[..., :, None] - a_cum[..., None, :]
    return jnp.exp(jnp.where(mask, diff, -jnp.inf))


def ssd_chunked(x, dt, a_head, bm, cm):
    b, t, h, p = x.shape
    g, n = bm.shape[-2:]
    c = t // CHUNK
    hg = h // g
    xdt = (x * dt[..., None]).reshape(b, c, CHUNK, g, hg, p)
    a_cum = jnp.cumsum(jnp.moveaxis((dt * a_head).reshape(b, c, CHUNK, g, hg), 2, -1), axis=-1)
    bc = bm.reshape(b, c, CHUNK, g, n)
    cc = cm.reshape(b, c, CHUNK, g, n)
    cb = jnp.einsum("bclgn,bcsgn->bcgls", cc, bc)
    scores = cb[:, :, :, None] * segsum_exp(a_cum)
    y_diag = jnp.einsum("bcgjls,bcsgjp->bclgjp", scores, xdt)
    decay_to_end = jnp.moveaxis(jnp.exp(a_cum[..., -1:] - a_cum), -1, 2)[..., None]
    states = jnp.einsum("bclgn,bclgjp->bcgjpn", bc, xdt * decay_to_end)
    chunk_decay = jnp.exp(a_cum[..., -1])

    def step(s, inp):
        st, dec = inp
        return s * dec[..., None, None] + st, s

    s0 = jnp.zeros((b, g, hg, p, n), jnp.float32)
    _, prev = lax.scan(step, s0, (jnp.moveaxis(states, 1, 0), jnp.moveaxis(chunk_decay, 1, 0)))
    prev = jnp.moveaxis(prev, 0, 1)
    y_off = jnp.einsum("bclgn,bcgjpn->bclgjp", cc, prev) * jnp.moveaxis(jnp.exp(a_cum), -1, 2)[..., None]
    return (y_diag + y_off).reshape(b, t, h, p)


def gated_delta_chunked(q, k, v, g, beta):
    b, t, h, kd = q.shape
    vd = v.shape[-1]
    c = t // CHUNK

    def chunks(a):
        return jnp.moveaxis(a.reshape(b, c, CHUNK, h, *a.shape[3:]), 3, 2)

    q, k, v, g, beta = chunks(q), chunks(k), chunks(v), chunks(g), chunks(beta)
    g_cum = jnp.cumsum(g, axis=-1)
    decay = segsum_exp(g_cum)
    k_beta = k * beta[..., None]
    strict = jnp.tril(jnp.ones((CHUNK, CHUNK), dtype=bool), -1)
    a_low = jnp.where(strict, jnp.einsum("bchlk,bchsk->bchls", k_beta, k) * decay, 0.0)
    eye = jnp.eye(CHUNK, dtype=jnp.float32)
    t_inv = lax.linalg.triangular_solve(a_low + eye, jnp.broadcast_to(eye, a_low.shape),
                                        left_side=True, lower=True, unit_diagonal=True)
    u = t_inv @ (v * beta[..., None])
    w = t_inv @ (k_beta * jnp.exp(g_cum)[..., None])
    qk = jnp.einsum("bchlk,bchsk->bchls", q, k) * decay
    q_dec = q * jnp.exp(g_cum)[..., None]
    k_dec = k * jnp.exp(g_cum[..., -1:] - g_cum)[..., None]
    last = jnp.exp(g_cum[..., -1])

    def step(s, inp):
        qk_i, u_i, w_i, qd_i, kd_i, last_i = inp
        v_new = u_i - jnp.einsum("bhlk,bhkv->bhlv", w_i, s)
        o = jnp.einsum("bhlk,bhkv->bhlv", qd_i, s) + jnp.einsum("bhls,bhsv->bhlv", qk_i, v_new)
        s = s * last_i[..., None, None] + jnp.einsum("bhlk,bhlv->bhkv", kd_i, v_new)
        return s, o

    s0 = jnp.zeros((b, h, kd, vd), jnp.float32)
    xs = (jnp.moveaxis(qk, 1, 0), jnp.moveaxis(u, 1, 0), jnp.moveaxis(w, 1, 0),
          jnp.moveaxis(q_dec, 1, 0), jnp.moveaxis(k_dec, 1, 0), jnp.moveaxis(last, 1, 0))
    _, o = lax.scan(step, s0, xs)
    o = jnp.moveaxis(jnp.moveaxis(o, 0, 1), 3, 2)
    return o.reshape(b, t, h, vd)


def hybrid_mixer(u, w_in, ssd_conv_w, ssd_conv_b, ssd_dt_bias, ssd_a_log, ssd_d, ssd_norm_w,
                 dn_conv_w, dn_dt_bias, dn_a_log, dn_norm_w, w_branch, w_out):
    f32 = jnp.float32
    b, l_in, _ = u.shape
    t = l_in + PAD
    up = jnp.pad(u, ((0, 0), (PAD, 0), (0, 0)))
    valid = (jnp.arange(t) >= PAD).astype(f32)[None, :, None]
    proj = up @ w_in
    z_s, xbc, dt_raw, qkv, a_raw, b_raw, z_d, gate_raw = jnp.split(
        proj, np.cumsum(IN_SPLITS)[:-1].tolist(), axis=-1)

    xbc = jax.nn.silu(causal_depthwise_conv(xbc, ssd_conv_w) + ssd_conv_b)
    xs, bm, cm = jnp.split(xbc, [SSD_D_INNER, SSD_D_INNER + SSD_GROUPS * SSD_STATE], axis=-1)
    dt = jax.nn.softplus(dt_raw.astype(f32) + ssd_dt_bias.astype(f32)) * valid
    a_head = -jnp.exp(ssd_a_log.astype(f32))
    xh = xs.astype(f32).reshape(b, t, SSD_HEADS, SSD_HEAD_DIM)
    y_s = ssd_chunked(xh, dt, a_head,
                      bm.astype(f32).reshape(b, t, SSD_GROUPS, SSD_STATE),
                      cm.astype(f32).reshape(b, t, SSD_GROUPS, SSD_STATE))
    y_s = (y_s + ssd_d.astype(f32)[:, None] * xh).reshape(b, t, SSD_D_INNER)
    y_s = group_rms(y_s * jax.nn.silu(z_s.astype(f32)), SSD_GROUPS) * ssd_norm_w.astype(f32)

    qkv = jax.nn.silu(causal_depthwise_conv(qkv, dn_conv_w))
    q, k, v = jnp.split(qkv, [DN_KEY_DIM, 2 * DN_KEY_DIM], axis=-1)
    rep = DN_V_HEADS // DN_QK_HEADS
    q = jnp.repeat(l2norm(q.astype(f32).reshape(b, t, DN_QK_HEADS, DN_HEAD_K)) * (DN_HEAD_K ** -0.5), rep, axis=2)
    k = jnp.repeat(l2norm(k.astype(f32).reshape(b, t, DN_QK_HEADS, DN_HEAD_K)), rep, axis=2)
    v = v.astype(f32).reshape(b, t, DN_V_HEADS, DN_HEAD_V)
    beta = jax.nn.sigmoid(b_raw.astype(f32)) * valid
    g = -jnp.exp(dn_a_log.astype(f32)) * jax.nn.softplus(a_raw.astype(f32) + dn_dt_bias.astype(f32)) * valid
    o = gated_delta_chunked(q, k, v, g, beta)
    y_d = (group_rms(o, 1) * dn_norm_w.astype(f32)).reshape(b, t, DN_VALUE_DIM) * jax.nn.silu(z_d.astype(f32))

    ys = jnp.stack([y_s, y_d], axis=2)[:, PAD:].astype(u.dtype)
    br = jnp.einsum("blnc,ncd->blnd", ys, w_branch)
    gates = jax.nn.sigmoid(gate_raw[:, PAD:].reshape(b, l_in, N_BRANCH, D_MODEL))
    merged = jnp.sum(gates * br, axis=2)
    return merged @ w_out


def swiglu(u, w_gate_up, w_down):
    gt, up = jnp.split(u @ w_gate_up, 2, axis=-1)
    return (jax.nn.silu(gt) * up) @ w_down


def _inv_softplus_dt(key, shape):
    dt = jnp.exp(jax.random.uniform(key, shape, jnp.float32) * (math.log(0.1) - math.log(0.001)) + math.log(0.001))
    return dt + jnp.log(-jnp.expm1(-dt))


def setup_inputs(seed: int = 0) -> dict:
    key = jax.random.key(seed)
    ks = jax.random.split(key, 24)
    nrm = lambda k, s, sc: jax.random.normal(k, s, jnp.float32) * sc
    gain = lambda k, s: 1.0 + 0.02 * jax.random.normal(k, s, jnp.float32)
    return {
        "x": nrm(ks[0], (BATCH, SEQ, D_MODEL), 1.0),
        "meta_tokens": nrm(ks[1], (N_META, D_MODEL), 1.0),
        "mix_norm_w": gain(ks[2], (DEPTH, D_MODEL)),
        "w_in": nrm(ks[3], (DEPTH, D_MODEL, D_IN_PROJ), D_MODEL ** -0.5),
        "ssd_conv_w": nrm(ks[4], (DEPTH, SSD_CONV, SSD_CONV_DIM), SSD_CONV ** -0.5),
        "ssd_conv_b": nrm(ks[5], (DEPTH, SSD_CONV_DIM), 0.02),
        "ssd_dt_bias": _inv_softplus_dt(ks[6], (DEPTH, SSD_HEADS)),
        "ssd_a_log": jnp.log(jax.random.uniform(ks[7], (DEPTH, SSD_HEADS), jnp.float32, 1.0, 16.0)),
        "ssd_d": gain(ks[8], (DEPTH, SSD_HEADS)),
        "ssd_norm_w": gain(ks[9], (DEPTH, SSD_D_INNER)),
        "dn_conv_w": nrm(ks[10], (DEPTH, DN_CONV, DN_CONV_DIM), DN_CONV ** -0.5),
        "dn_dt_bias": _inv_softplus_dt(ks[11], (DEPTH, DN_V_HEADS)),
        "dn_a_log": jnp.log(jax.random.uniform(ks[12], (DEPTH, DN_V_HEADS), jnp.float32, 1.0, 16.0)),
        "dn_norm_w": gain(ks[13], (DEPTH, DN_HEAD_V)),
        "w_branch": nrm(ks[14], (DEPTH, N_BRANCH, BRANCH_WIDTH, D_MODEL), BRANCH_WIDTH ** -0.5),
        "w_out": nrm(ks[15], (DEPTH, D_MODEL, D_MODEL), D_MODEL ** -0.5),
        "ffn_norm_w": gain(ks[16], (DEPTH, D_MODEL)),
        "w_gate_up": nrm(ks[17], (DEPTH, D_MODEL, 2 * D_FF), D_MODEL ** -0.5),
        "w_down": nrm(ks[18], (DEPTH, D_FF, D_MODEL), D_FF ** -0.5),
        "final_norm_w": gain(ks[19], (D_MODEL,)),
    }


def reference(x, meta_tokens, mix_norm_w, w_in, ssd_conv_w, ssd_conv_b, ssd_dt_bias, ssd_a_log,
              ssd_d, ssd_norm_w, dn_conv_w, dn_dt_bias, dn_a_log, dn_norm_w, w_branch, w_out,
              ffn_norm_w, w_gate_up, w_down, final_norm_w):
    b = x.shape[0]
    meta = jnp.broadcast_to(meta_tokens.astype(x.dtype)[None], (b, N_META, D_MODEL))
    h = jnp.concatenate([meta, x], axis=1)
    for i in range(DEPTH):
        h = h + hybrid_mixer(rmsnorm(h, mix_norm_w[i]), w_in[i], ssd_conv_w[i], ssd_conv_b[i],
                             ssd_dt_bias[i], ssd_a_log[i], ssd_d[i], ssd_norm_w[i], dn_conv_w[i],
                             dn_dt_bias[i], dn_a_log[i], dn_norm_w[i], w_branch[i], w_out[i])
        h = h + swiglu(rmsnorm(h, ffn_norm_w[i]), w_gate_up[i], w_down[i])
    return rmsnorm(h, final_norm_w)[:, N_META:]
```

```python
import os
import numpy as np
import concourse.bass as bass
import concourse.mybir as mybir
from concourse.bass_utils import run_bass_kernel_spmd

F32 = mybir.dt.float32
BF16 = mybir.dt.bfloat16
AF = mybir.ActivationFunctionType
ALU = mybir.AluOpType
AX = mybir.AxisListType

D = 1024
EPS = 1e-6
NEG = -30000.0
DFF = 2816
NCM = 1792
NTM = 1040
NCONST = 128 * 6 + 64 * 2 + 512 + 256 + 256
GROUPS = [[0, 1, 2, 3], [4, 5, 6, 7]]


class Trk:
    __slots__ = ("w", "r", "ps")

    def __init__(self, ps=False):
        self.w = {}
        self.r = {}
        self.ps = ps


class Sched:
    def __init__(self, nc):
        self.nc = nc
        self.prog = {k: [] for k in ("pe", "dve", "act", "pool", "sp")}
        self.cnt = {k: 0 for k in self.prog}
        self.seen = {k: {} for k in self.prog}
        self.sems = {}
        self.dma_cnt = {}
        self._cms = []
        self.log = {k: [] for k in self.prog}
        self.wkeys = {}

    def sem(self, key):
        if key not in self.sems:
            cm = self.nc.semaphore("s_" + key)
            self.sems[key] = cm.__enter__()
            self._cms.append(cm)
        return self.sems[key]

    def _waits(self, eng, reads, writes):
        self._lastw = []
        need = {}

        def add(k, v, raw=False):
            if (k != eng or (raw and eng != "pe")) and need.get(k, 0) < v:
                need[k] = v
        for t in reads:
            for k, v in t.w.items():
                add(k, v, True)
            if t.ps:
                for k, v in t.r.items():
                    add(k, v)
        for t in writes:
            for k, v in t.w.items():
                add(k, v)
            for k, v in t.r.items():
                add(k, v)
        out = []
        for k, v in need.items():
            if self.seen[eng].get(k, 0) < v:
                self.seen[eng][k] = v
                out.append((self.sem(k if k.startswith("d_") else "e_" + k), v))
                self._lastw.append((k if k.startswith("d_") else "e_" + k, v))
        return out

    def _mark(self, key, v, reads, writes):
        for t in reads:
            if t.r.get(key, 0) < v:
                t.r[key] = v
        for t in writes:
            t.w[key] = v
            t.r = {}

    def op(self, eng, fn, reads=(), writes=()):
        wl = self._waits(eng, reads, writes)
        self.cnt[eng] += 1
        semh = self.sem("e_" + eng)

        def emit(e, fn=fn, wl=wl, semh=semh):
            for s, v in wl:
                e.wait_ge(s, v)
            fn(e).then_inc(semh, 1)
        self.prog[eng].append(emit)
        self.log[eng].append((list(self._lastw), ("e_" + eng, 1)))
        self._mark(eng, self.cnt[eng], reads, writes)

    def dma(self, eng, fn, semkey, reads=(), writes=(), inc=16):
        key = "d_" + semkey
        wl = self._waits(eng, reads, writes)
        self.dma_cnt[key] = self.dma_cnt.get(key, 0) + inc
        v = self.dma_cnt[key]
        semh = self.sem(key)

        def emit(e, fn=fn, wl=wl, semh=semh, inc=inc):
            for s, vv in wl:
                e.wait_ge(s, vv)
            if inc == 1:
                fn(e).then_inc(semh)
            else:
                fn(e).then_inc(semh, inc)
        self.prog[eng].append(emit)
        self.log[eng].append((list(self._lastw), (key, inc)))
        self._mark(key, v, reads, writes)
        return (key, v)

    def final_wait(self, eng, deps):
        wl = [(self.sem(k), v) for k, v in deps]

        def emit(e, wl=wl):
            for s, v in wl:
                e.wait_ge(s, v)
        self.prog[eng].append(emit)

    def emit_all(self):
        nc = self.nc
        with nc.Block() as block:
            @block.tensor
            def _(e):
                for f in self.prog["pe"]:
                    f(e)

            @block.vector
            def _(e):
                for f in self.prog["dve"]:
                    f(e)

            @block.scalar
            def _(e):
                for f in self.prog["act"]:
                    f(e)

            @block.gpsimd
            def _(e):
                for f in self.prog["pool"]:
                    f(e)

            @block.sync
            def _(e):
                for f in self.prog["sp"]:
                    f(e)

    def close(self):
        for cm in reversed(self._cms):
            cm.__exit__(None, None, None)


class Buf:
    def __init__(self, t):
        self.t = t
        self.k = Trk()

    def __getitem__(self, idx):
        return self.t[idx]


def make_consts():
    c = np.zeros((128, NCONST), np.float32)
    t = np.arange(128)
    tc, tp = t // 64, t % 64
    same = (tc[:, None] == tc[None, :]).astype(np.float32)
    o = 0
    c[:, o:o + 128] = np.eye(128); o += 128
    c[:, o:o + 128] = same; o += 128
    c[:, o:o + 128] = -same * (tp[:, None] <= tp[None, :]); o += 128
    c[:, o:o + 128] = same * (tp[:, None] <= tp[None, :]); o += 128
    c[:, o:o + 128] = -same; o += 128
    c[:, o:o + 128] = same * (tp[:, None] > tp[None, :]); o += 128
    l = np.arange(64)
    c[:, o:o + 64] = (tp[:, None] <= l[None, :]); o += 64
    c[:, o:o + 64] = (tp[:, None] == l[None, :]); o += 64
    mA = NEG * (tp[:, None] > l[None, :]).astype(np.float32)
    c[:, o:o + 512] = np.tile(mA, (1, 8)); o += 512
    mN = NEG * (l[None, :] >= tp[:, None]).astype(np.float32)
    c[:, o:o + 256] = np.tile(mN, (1, 4)); o += 256
    c[0:64, o:o + 128] = 1.0; o += 128
    c[64:128, o:o + 128] = 1.0; o += 128
    assert o == NCONST
    return c


C_ID, C_BD, C_NLE, C_LE, C_NBD, C_GT = [128 * i for i in range(6)]
C_U2 = 768
C_I2 = 832
C_MA = 896
C_MN = 1408
C_C0 = 1664


class _Stop(Exception):
    pass


def build(NU, stop=None):
    ckc = {}

    def ck(name):
        ckc[name] = ckc.get(name, 0) + 1
        if stop == name or stop == '%s#%d' % (name, ckc[name]):
            raise _Stop()
    NTA = 1 + 4 * NU
    TOKA = 128 * NTA
    NBLK = NU // 4
    nc = bass.Bass("TRN2", target_bir_lowering=False)
    dt_in = lambda n, s: nc.dram_tensor(n, s, F32, kind="ExternalInput").ap()
    xin = dt_in("xin", [TOKA, D])
    wcm_d = dt_in("wcm", [D, NCM])
    wtm_d = dt_in("wtm", [D, NTM])
    cw_d = dt_in("cw", [128, 14 * 5])
    nw_d = dt_in("nw", [128, 24])
    nwr_d = dt_in("nwr", [1, D])
    rowp_d = dt_in("rowp", [1, 672])
    const_d = dt_in("consts", [128, NCONST])
    QB = [(0, 6), (6, 6), (12, 5), (17, 5)]
    if stop is None or stop in ('mix', 'phaseA'):
        wgm_d = dt_in("wgm", [8, 128, 8 * 2 * 128])
        wbrm_d = dt_in("wbrm", [8, 128, 32 * 128])
        wout_d = dt_in("wout", [D, D])
        wguq_d = [dt_in("wguq%d" % i, [128, 8 * 2 * nb * 128]) for i, (b0, nb) in enumerate(QB)]
        wdnq_d = [dt_in("wdnq%d" % i, [128, nb * D]) for i, (b0, nb) in enumerate(QB)]
    out_d = nc.dram_tensor("out", [NBLK * 512, D], F32, kind="ExternalOutput").ap()
    ysrc = [nc.dram_tensor("ysrc%d" % u, [1024, 512], BF16) for u in range(NU)]
    yround = [nc.dram_tensor("yround%d" % r, [4 * 4096, 512], BF16) for r in range(NBLK)]

    S = Sched(nc)
    stack = []
    DBG = bool(os.environ.get("KDBG"))
    TAPT = int(os.environ.get("TAPT", "1"))
    taps = {}

    def tap(name, ap, trk, shape, dt=F32):
        if not DBG or name in taps:
            return
        d = nc.dram_tensor("dbg_" + name, shape, dt, kind="ExternalOutput").ap()
        taps[name] = S.dma("sp", lambda e: e.dma_start(out=d, in_=ap), "tap_" + name, reads=[trk])

    epoch = {}

    def sb(name, shape, dt=F32):
        cm = nc.sbuf_tensor("sb_" + name, shape, dt, align_bytes=64)
        t = cm.__enter__()
        stack.append(cm)
        b_ = Buf(t)
        b_.k.r = dict(epoch)
        return b_

    def new_epoch():
        for k_ in ("pe", "dve", "act"):
            epoch[k_] = S.cnt[k_]
        for k_, v_ in S.dma_cnt.items():
            epoch[k_] = v_

    banks = []
    for i in range(8):
        cm = nc.psum_tensor("ps%d" % i, [128, 512], F32)
        banks.append(Buf(cm.__enter__()))
        banks[-1].k.ps = True
        stack.append(cm)
    bank_i = [0]

    def PS():
        b = banks[bank_i[0] % 8]
        bank_i[0] += 1
        return b

    consts = sb("consts", [128, NCONST])
    nw = sb("nw", [128, 3, 8])
    nwrow = sb("nwrow", [128, D])
    ident_b = sb("identb", [128, 128], BF16)
    S.dma("sp", lambda e: e.dma_start(out=consts[:, :], in_=const_d[:, :]), "c0", writes=[consts.k])
    S.dma("sp", lambda e: e.dma_start(out=nw[:, :, :].rearrange("p a k -> p (a k)"), in_=nw_d[:, :]), "c1",
          writes=[nw.k])
    S.dma("sp", lambda e: e.dma_start(out=nwrow[:, :], in_=nwr_d[0:1, :].partition_broadcast(128)), "c2",
          writes=[nwrow.k])
    S.op("dve", lambda e: e.tensor_copy(ident_b[:, :], consts[:, C_ID:C_ID + 128]), reads=[consts.k],
         writes=[ident_b.k])
    cst = lambda o, n=128: consts[:, o:o + n]

    def norm_transpose(xt, xt_k, xn, sq, st, dstT, dst_k, col0, widx):
        S.op("act", lambda e: e.activation(sq[:, :], xt, AF.Square, accum_out=st[:, 0:1]),
             reads=[xt_k], writes=[sq.k, st.k])
        S.op("act", lambda e: e.activation(st[:, 1:2], st[:, 0:1], AF.Ln, bias=EPS, scale=1.0 / D),
             reads=[st.k], writes=[st.k])
        S.op("act", lambda e: e.activation(st[:, 2:3], st[:, 1:2], AF.Exp, scale=-0.5), reads=[st.k], writes=[st.k])
        S.op("dve", lambda e: e.tensor_scalar(xn[:, :], xt, st[:, 2:3], None, ALU.mult),
             reads=[st.k, xt_k], writes=[xn.k])
        p = PS()
        pb = p.t[:, :].bitcast(BF16)
        for k in range(8):
            S.op("pe", lambda e, k=k: e.transpose(pb[:, k * 128:(k + 1) * 128], xn[:, k * 128:(k + 1) * 128],
                                                  ident_b[:, :]),
                 reads=[xn.k, ident_b.k], writes=[p.k])
        S.op("dve", lambda e: e.tensor_tensor(
            dstT[:, :, col0:col0 + 128], pb.rearrange("p (k t) -> p k t", k=8),
            nw[:, widx, :].unsqueeze(2).broadcast_to([128, 8, 128]), ALU.mult),
            reads=[p.k, nw.k], writes=[dst_k])

    try:
        mark_a = len(stack)
        wcm = sb("wcm", [128, 8, NCM], BF16)
        wtm = sb("wtm", [128, 8, NTM], BF16)
        for k in range(8):
            S.dma("pool", lambda e, k=k: e.dma_start(out=wcm[:, k, :], in_=wcm_d[k * 128:(k + 1) * 128, :]),
                  "w%d" % (k % 4), writes=[wcm.k])
            S.dma("pool", lambda e, k=k: e.dma_start(out=wtm[:, k, :], in_=wtm_d[k * 128:(k + 1) * 128, :]),
                  "w%d" % (k % 4), writes=[wtm.k])
        for kk_ in range(4):
            wcm.k.w["d_w%d" % kk_] = S.dma_cnt["d_w%d" % kk_]
            wtm.k.w["d_w%d" % kk_] = S.dma_cnt["d_w%d" % kk_]
        cw = sb("cw", [128, 14, 5])
        S.dma("sp", lambda e: e.dma_start(out=cw[:, :, :].rearrange("p b k -> p (b k)"), in_=cw_d[:, :]), "c3",
              writes=[cw.k])
        rowp = sb("rowp", [128, 672])
        S.dma("sp", lambda e: e.dma_start(out=rowp[:, :], in_=rowp_d[0:1, :].partition_broadcast(128)), "c4",
              writes=[rowp.k])
        negA = sb("negA", [128, 12])
        S.op("act", lambda e: e.activation(negA[:, :], rowp[:, 12:24], AF.Exp), reads=[rowp.k], writes=[negA.k])
        S.op("dve", lambda e: e.tensor_scalar(negA[:, :], negA[:, :], -1.0, None, ALU.mult), reads=[negA.k],
             writes=[negA.k])

        xt2 = [sb("xt%d" % i, [128, D]) for i in range(2)]
        xn = sb("xn", [128, D], BF16)
        sq = sb("sq", [128, D], BF16)
        st = sb("st", [128, 4])
        xnT = sb("xnT", [128, 8, 512], BF16)
        halo = sb("halo", [128, 14, 3])
        pre = [sb("pre%d" % i, [128, 515]) for i in range(2)]
        acc = [sb("acc%d" % i, [128, 512]) for i in range(2)]
        cma = [sb("cma%d" % i, [128, 512]) for i in range(14)]
        rs = sb("rs", [128, 512])
        zs = [sb("zs0", [128, 512])] * 2
        zd = [sb("zd%d" % i, [128, 512]) for i in range(2)]
        hand = {}
        sm = [sb("sm%d" % i, [128, 16]) for i in range(2)]
        aall = [sb("aall%d" % i, [128, 12]) for i in range(2)]
        S.op("dve", lambda e: e.memset(halo[:, :, :], 0.0), writes=[halo.k])

        x_tok = sb("x_tok", [128, 512])
        xdt = sb("xdt", [128, 512], BF16)
        xdtd2 = [sb("xdtd%d" % i, [128, 512], BF16) for i in range(2)]
        b_tok = sb("b_tok", [128, 128], BF16)
        k_tok = sb("k_tok", [128, 256])
        vb = sb("vb", [128, 512], BF16)
        kbg = sb("kbg", [128, 512], BF16)
        kdec = sb("kdec", [128, 512], BF16)
        rhs1 = sb("rhs1", [128, 768])
        rhs2 = sb("rhs2", [128, 768])
        rhsb = sb("rhsb", [128, 256])
        LT = sb("LT", [128, 512])
        DT = sb("DT", [128, 256])
        Dn = sb("Dn", [128, 256])
        esm = sb("esm", [128, 24])
        cdc = [sb("cd%d" % i, [128, 12]) for i in range(2)]
        bg = sb("bg", [128, 4])
        scT = sb("scT", [128, 8, 128], BF16)
        t2 = sb("t2", [128, 256])
        t3 = sb("t3", [128, 256])
        Qm = [sb("Qm%d" % i, [128, 4, 128], BF16) for i in range(2)]
        Pm = [sb("Pm%d" % i, [128, 4, 128], BF16) for i in range(2)]
        Xm = [sb("Xm%d" % i, [128, 4, 128], BF16) for i in range(2)]
        qkT = sb("qkT", [128, 4, 128], BF16)
        u_sb = sb("u_sb", [128, 512])
        wT = sb("wT", [128, 4, 128])
        vnew2 = [sb("vnew%d" % i, [128, 512], BF16) for i in range(2)]
        o_sb = sb("o_sb", [128, 512])
        t1 = sb("t1", [128, 512])
        y1 = sb("y1", [128, 512])
        y2 = sb("y2", [128, 512])
        yb = sb("yb", [128, 1024], BF16)
        st2 = sb("st2", [128, 16])
        Sss = [sb("Sss%d" % i, [128, 512]) for i in range(2)]
        Sdn = [sb("Sdn%d" % i, [128, 512]) for i in range(2)]
        yT = [sb("yT0", [128, 8, 512], BF16)] * 2
        for b_ in (scT, qkT, Qm[0], Qm[1], Pm[0], Pm[1], Xm[0], Xm[1]):
            S.op("dve", lambda e, b_=b_: e.memset(b_[:, :, :], 0.0), writes=[b_.k])
        for b_ in (xdtd2[0], xdtd2[1], vnew2[0], vnew2[1]):
            S.op("dve", lambda e, b_=b_: e.memset(b_[:, :], 0.0), writes=[b_.k])
        S.op("dve", lambda e: e.memset(Sss[0][:, :], 0.0), writes=[Sss[0].k])
        S.op("dve", lambda e: e.memset(Sdn[0][:, :], 0.0), writes=[Sdn[0].k])
        s_par = [0]

        HV = lambda ap, h, n: ap.rearrange("p (h n) -> p h n", h=h)

        def phase_a_tile(ti, stl, col0):
            par = ti % 2
            tk = slice(col0, col0 + 128)
            for (dst, c0, w) in ((zs[par], 0, 512), (zd[par], 512, 512)):
                p = PS()
                for k in range(8):
                    S.op("pe", lambda e, k=k, p=p, c0=c0, w=w: e.matmul(p[:, 0:w], lhsT=xnT[:, k, tk],
                                                                        rhs=wtm[:, k, c0:c0 + w], start=(k == 0),
                                                                        stop=(k == 7)),
                         reads=[xnT.k, wtm.k], writes=[p.k])
                S.op("act", lambda e, p=p, dst=dst: e.activation(dst[:, :], p[:, 0:512], AF.Silu), reads=[p.k],
                     writes=[dst.k])
            ck('t_z')
            p = PS()
            for k in range(8):
                S.op("pe", lambda e, k=k, p=p: e.matmul(p[:, 0:16], lhsT=xnT[:, k, tk], rhs=wtm[:, k, 1024:1040],
                                                        start=(k == 0), stop=(k == 7)),
                     reads=[xnT.k, wtm.k], writes=[p.k])
            smt, aat = sm[par], aall[par]
            S.op("dve", lambda e: e.tensor_tensor(smt[:, 0:12], p[:, 0:12], rowp[:, 0:12], ALU.add),
                 reads=[p.k, rowp.k], writes=[smt.k])
            S.op("act", lambda e: e.activation(smt[:, 0:12], smt[:, 0:12], AF.Exp), reads=[smt.k], writes=[smt.k])
            S.op("act", lambda e: e.activation(smt[:, 0:12], smt[:, 0:12], AF.Ln, bias=1.0, scale=1.0),
                 reads=[smt.k], writes=[smt.k])
            S.op("act", lambda e: e.activation(smt[:, 12:16], p[:, 12:16], AF.Exp, scale=-1.0), reads=[p.k],
                 writes=[smt.k])
            S.op("dve", lambda e: e.tensor_scalar(smt[:, 12:16], smt[:, 12:16], 1.0, None, ALU.add), reads=[smt.k],
                 writes=[smt.k])
            S.op("dve", lambda e: e.reciprocal(smt[:, 12:16], smt[:, 12:16]), reads=[smt.k], writes=[smt.k])
            if ti == 0 and not os.environ.get('NOMS'):
                S.op("dve", lambda e: e.memset(smt[0:112, :], 0.0), reads=[smt.k], writes=[smt.k])
            S.op("dve", lambda e: e.tensor_tensor(aat[:, :], smt[:, 0:12], negA[:, :], ALU.mult),
                 reads=[smt.k, negA.k], writes=[aat.k])
            ck('t_small')
            px = PS()
            for i in range(4):
                S.op("pe", lambda e, i=i: e.transpose(px[:, i * 128:(i + 1) * 128], cma[i][:, tk], cst(C_ID)),
                     reads=[cma[i].k, consts.k], writes=[px.k])
            S.op("act", lambda e: e.copy(x_tok[:, :], px[:, :]), reads=[px.k], writes=[x_tok.k])
            S.op("dve", lambda e: e.tensor_tensor(HV(xdt[:, :], 8, 64), HV(px[:, :], 8, 64),
                                                  smt[:, 0:8].unsqueeze(2).broadcast_to([128, 8, 64]), ALU.mult),
                 reads=[px.k, smt.k] + ([x_tok.k] if os.environ.get('SER') else []), writes=[xdt.k])
            ck('t_px')
            pk = PS()
            for i, blk in enumerate((4, 8, 9)):
                S.op("pe", lambda e, i=i, blk=blk: e.transpose(pk[:, i * 128:(i + 1) * 128], cma[blk][:, tk],
                                                               cst(C_ID)),
                     reads=[cma[blk].k, consts.k], writes=[pk.k])
            S.op("act", lambda e: e.copy(b_tok[:, :], pk[:, 0:128]), reads=[pk.k], writes=[b_tok.k])
            S.op("act", lambda e: e.copy(k_tok[:, :], pk[:, 128:384]), reads=[pk.k], writes=[k_tok.k])
            ck('t_pk')
            pv = PS()
            for i in range(4):
                S.op("pe", lambda e, i=i: e.transpose(pv[:, i * 128:(i + 1) * 128], cma[10 + i][:, tk], cst(C_ID)),
                     reads=[cma[10 + i].k, consts.k], writes=[pv.k])
            S.op("dve", lambda e: e.tensor_tensor(HV(vb[:, :], 4, 128), HV(pv[:, :], 4, 128),
                                                  smt[:, 12:16].unsqueeze(2).broadcast_to([128, 4, 128]), ALU.mult),
                 reads=[pv.k, smt.k], writes=[vb.k])
            if ti == TAPT:
                tap('smt', smt[:, :], smt.k, [128, 16])
                tap('aat', aat[:, :], aat.k, [128, 12])
                tap('zs', zs[0][:, :], zs[0].k, [128, 512])
                tap('xdt', xdt[:, :], xdt.k, [128, 512])
                tap('vb', vb[:, :], vb.k, [128, 512])
                tap('k_tok', k_tok[:, :], k_tok.k, [128, 256])
            ck('t_tr')
            yield
            S.op("dve", lambda e: e.tensor_tensor(HV(rhs1[:, :], 12, 64),
                                                  aat[:, :].unsqueeze(2).broadcast_to([128, 12, 64]),
                                                  cst(C_U2, 64).unsqueeze(1).broadcast_to([128, 12, 64]), ALU.mult),
                 reads=[aat.k, consts.k], writes=[rhs1.k])
            S.op("dve", lambda e: e.tensor_copy(HV(rhs2[:, :], 12, 64),
                                                aat[:, :].unsqueeze(2).broadcast_to([128, 12, 64])),
                 reads=[aat.k], writes=[rhs2.k])
            pL = PS()
            S.op("pe", lambda e: e.matmul(pL[:, :], lhsT=cst(C_BD), rhs=rhs1[:, 0:512], start=True, stop=False),
                 reads=[rhs1.k, consts.k], writes=[pL.k])
            S.op("pe", lambda e: e.matmul(pL[:, :], lhsT=cst(C_NLE), rhs=rhs2[:, 0:512], start=False, stop=False),
                 reads=[rhs2.k], writes=[pL.k])
            S.op("pe", lambda e: e.matmul(pL[:, :], lhsT=cst(C_ID), rhs=cst(C_MA, 512), start=False, stop=True),
                 reads=[consts.k], writes=[pL.k])
            S.op("act", lambda e: e.activation(LT[:, :], pL[:, :], AF.Exp), reads=[pL.k], writes=[LT.k])
            pD = PS()
            S.op("pe", lambda e: e.matmul(pD[:, 0:256], lhsT=cst(C_BD), rhs=rhs1[:, 512:768], start=True, stop=False),
                 reads=[rhs1.k, consts.k], writes=[pD.k])
            S.op("pe", lambda e: e.matmul(pD[:, 0:256], lhsT=cst(C_NLE), rhs=rhs2[:, 512:768], start=False,
                                          stop=False), reads=[rhs2.k], writes=[pD.k])
            S.op("pe", lambda e: e.matmul(pD[:, 0:256], lhsT=cst(C_ID), rhs=cst(C_MA, 256), start=False, stop=True),
                 reads=[consts.k], writes=[pD.k])
            S.op("pe", lambda e: e.matmul(pD[:, 256:512], lhsT=cst(C_LE), rhs=rhs2[:, 512:768], start=True,
                                          stop=False), reads=[rhs2.k], writes=[pD.k])
            S.op("pe", lambda e: e.matmul(pD[:, 256:512], lhsT=cst(C_NBD), rhs=rhs1[:, 512:768], start=False,
                                          stop=False), reads=[rhs1.k], writes=[pD.k])
            S.op("pe", lambda e: e.matmul(pD[:, 256:512], lhsT=cst(C_ID), rhs=cst(C_MN, 256), start=False, stop=True),
                 reads=[consts.k], writes=[pD.k])
            S.op("act", lambda e: e.activation(DT[:, :], pD[:, 0:256], AF.Exp), reads=[pD.k], writes=[DT.k])
            S.op("act", lambda e: e.activation(Dn[:, :], pD[:, 256:512], AF.Exp), reads=[pD.k], writes=[Dn.k])
            ck('t_dec')
            pS = PS()
            S.op("pe", lambda e: e.matmul(pS[:, 0:12], lhsT=cst(C_LE), rhs=aat[:, :], start=True, stop=True),
                 reads=[aat.k, consts.k], writes=[pS.k])
            S.op("pe", lambda e: e.matmul(pS[:, 12:24], lhsT=cst(C_GT), rhs=aat[:, :], start=True, stop=True),
                 reads=[aat.k, consts.k], writes=[pS.k])
            S.op("act", lambda e: e.activation(esm[:, :], pS[:, 0:24], AF.Exp), reads=[pS.k], writes=[esm.k])
            if ti == TAPT:
                tap('LT', LT[:, :], LT.k, [128, 512])
                tap('DT', DT[:, :], DT.k, [128, 256])
                tap('Dn', Dn[:, :], Dn.k, [128, 256])
                tap('esm', esm[:, :], esm.k, [128, 24])
            ck('t_sm')
            pcb = PS()
            S.op("pe", lambda e: e.matmul(pcb[:, 0:128], lhsT=cma[4][:, tk], rhs=cma[5][:, tk], start=True, stop=True),
                 reads=[cma[4].k, cma[5].k], writes=[pcb.k])
            for hf in range(2):
                r = slice(64 * hf, 64 * hf + 64)
                S.op("dve", lambda e, r=r: e.tensor_tensor(
                    scT[r, :, r], HV(LT[r, :], 8, 64), pcb[r, r].unsqueeze(1).broadcast_to([64, 8, 64]), ALU.mult),
                    reads=[LT.k, pcb.k], writes=[scT.k])
            pY = PS()
            for j in range(8):
                S.op("pe", lambda e, j=j: e.matmul(pY[:, j * 64:(j + 1) * 64], lhsT=scT[:, j, :],
                                                   rhs=xdt[:, j * 64:(j + 1) * 64], start=True, stop=True),
                     reads=[scT.k, xdt.k], writes=[pY.k])
            hand[ti] = pY

        ones_f = sb("ones_f", [128, 128])
        S.op("dve", lambda e: e.memset(ones_f[:, :], 1.0), writes=[ones_f.k])

        def phase_a_tile2(ti, col0, u_idx, ucol):
            par = ti % 2
            pY = hand[ti]
            tk = slice(col0, col0 + 128)
            smt, aat = sm[par], aall[par]
            pC = PS()
            for c in range(2):
                r = slice(64 * c, 64 * c + 64)
                S.op("pe", lambda e, c=c, r=r: e.matmul(pC[:, 16 * c:16 * c + 12], lhsT=cst(C_C0 + 128 * c), rhs=aat[:, :],
                                                        start=True, stop=True),
                     reads=[aat.k, consts.k], writes=[pC.k])
            for c in range(2):
                S.op("act", lambda e, c=c: e.activation(cdc[c][:, :], pC[:, 16 * c:16 * c + 12], AF.Exp),
                     reads=[pC.k], writes=[cdc[c].k])
            ck('a_cd')
            for c in range(2):
                r = slice(64 * c, 64 * c + 64)
                S.op("dve", lambda e, c=c, r=r: e.tensor_tensor(
                    HV(xdtd2[c][r, :], 8, 64), HV(xdt[r, :], 8, 64),
                    esm[r, 12:20].unsqueeze(2).broadcast_to([64, 8, 64]), ALU.mult),
                    reads=[xdt.k, esm.k], writes=[xdtd2[c].k])
            k4 = k_tok[:, :].rearrange("p (e k) -> p e k", e=2).unsqueeze(2).broadcast_to([128, 2, 2, 128])
            S.op("dve", lambda e: e.tensor_tensor(
                kdec[:, :].rearrange("p (e h k) -> p e h k", e=2, h=2), k4,
                esm[:, 20:24].rearrange("p (e h) -> p e h", e=2).unsqueeze(3).broadcast_to([128, 2, 2, 128]), ALU.mult),
                reads=[k_tok.k, esm.k], writes=[kdec.k])
            S.op("dve", lambda e: e.tensor_tensor(bg[:, :], smt[:, 12:16], esm[:, 8:12], ALU.mult),
                 reads=[smt.k, esm.k], writes=[bg.k])
            S.op("dve", lambda e: e.tensor_tensor(
                kbg[:, :].rearrange("p (e h k) -> p e h k", e=2, h=2), k4,
                bg[:, :].rearrange("p (e h) -> p e h", e=2).unsqueeze(3).broadcast_to([128, 2, 2, 128]), ALU.mult),
                reads=[k_tok.k, bg.k], writes=[kbg.k])
            sp0 = s_par[0]
            pO = [PS(), PS()]
            for c in range(2):
                r = slice(64 * c, 64 * c + 64)
                Sin, Sout = Sss[(sp0 + c) % 2], Sss[(sp0 + c + 1) % 2]
                S.op("pe", lambda e, c=c, Sin=Sin: e.matmul(pO[c][:, :], lhsT=cma[5][:, tk], rhs=Sin[:, :], start=True,
                                                            stop=True), reads=[cma[5].k, Sin.k], writes=[pO[c].k])
                pst = PS()
                S.op("pe", lambda e, c=c, pst=pst: e.matmul(pst[:, :], lhsT=b_tok[:, :], rhs=xdtd2[c][:, :], start=True,
                                                            stop=True), reads=[b_tok.k, xdtd2[c].k], writes=[pst.k])
                S.op("dve", lambda e, c=c, Sin=Sin, Sout=Sout: e.tensor_tensor(
                    HV(Sout[:, :], 8, 64), HV(Sin[:, :], 8, 64),
                    cdc[c][:, 0:8].unsqueeze(2).broadcast_to([128, 8, 64]), ALU.mult),
                    reads=[Sin.k, cdc[c].k], writes=[Sout.k])
                S.op("dve", lambda e, Sout=Sout, pst=pst: e.tensor_tensor(Sout[:, :], Sout[:, :], pst[:, :], ALU.add),
                     reads=[pst.k, Sout.k], writes=[Sout.k])
                S.op("dve", lambda e, c=c, r=r: e.tensor_tensor(
                    HV(t1[r, :], 8, 64), HV(pO[c][r, :], 8, 64),
                    esm[r, 0:8].unsqueeze(2).broadcast_to([64, 8, 64]), ALU.mult),
                    reads=[pO[c].k, esm.k], writes=[t1.k])
            S.op("dve", lambda e: e.tensor_tensor(y1[:, :], t1[:, :], pY[:, :], ALU.add), reads=[t1.k, pY.k],
                 writes=[y1.k])
            S.op("dve", lambda e: e.tensor_tensor(HV(y2[:, :], 8, 64), HV(x_tok[:, :], 8, 64),
                                                  rowp[:, 24:32].unsqueeze(2).broadcast_to([128, 8, 64]), ALU.mult),
                 reads=[x_tok.k, rowp.k], writes=[y2.k])
            S.op("dve", lambda e: e.tensor_tensor(y1[:, :], y1[:, :], y2[:, :], ALU.add), reads=[y1.k, y2.k],
                 writes=[y1.k])
            S.op("dve", lambda e: e.tensor_tensor(y1[:, :], y1[:, :], zs[par][:, :], ALU.mult), reads=[y1.k, zs[par].k],
                 writes=[y1.k])
            S.op("dve", lambda e: e.scalar_tensor_tensor(y2[:, :], y1[:, :], 1.0, y1[:, :], ALU.mult, ALU.mult,
                                                         accum_out=st2[:, 0:1]), reads=[y1.k], writes=[y2.k, st2.k])
            S.op("act", lambda e: e.activation(st2[:, 1:2], st2[:, 0:1], AF.Ln, bias=EPS, scale=1.0 / 512),
                 reads=[st2.k], writes=[st2.k])
            S.op("act", lambda e: e.activation(st2[:, 2:3], st2[:, 1:2], AF.Exp, scale=-0.5), reads=[st2.k],
                 writes=[st2.k])
            S.op("dve", lambda e: e.scalar_tensor_tensor(yb[:, 0:512], y1[:, :], st2[:, 2:3], rowp[:, 32:544], ALU.mult,
                                                         ALU.mult), reads=[y1.k, st2.k, rowp.k], writes=[yb.k])
            if ti == TAPT:
                tap('ys', yb[:, 0:512], yb.k, [128, 512], BF16)
                tap('y1', y1[:, :], y1.k, [128, 512])
                tap('Sss', Sss[sp0][:, :], Sss[sp0].k, [128, 512])
            ck('a_ssd')
            rb_ = rhsb
            S.op("dve", lambda e: e.tensor_tensor(HV(rb_[:, :], 4, 64),
                                                  cst(C_I2, 64).unsqueeze(1).broadcast_to([128, 4, 64]),
                                                  smt[:, 12:16].unsqueeze(2).broadcast_to([128, 4, 64]), ALU.mult),
                 reads=[smt.k, consts.k], writes=[rb_.k])
            pB = PS()
            S.op("pe", lambda e: e.matmul(pB[:, 0:256], lhsT=cst(C_GT), rhs=rb_[:, :], start=True, stop=True),
                 reads=[rb_.k, consts.k], writes=[pB.k])
            pK = PS()
            for e_ in range(2):
                S.op("pe", lambda e, e_=e_: e.matmul(pK[:, e_ * 256:e_ * 256 + 128], lhsT=cma[8 + e_][:, tk],
                                                     rhs=cma[8 + e_][:, tk], start=True, stop=True),
                     reads=[cma[8 + e_].k], writes=[pK.k])
                S.op("pe", lambda e, e_=e_: e.matmul(pK[:, e_ * 256 + 128:e_ * 256 + 256], lhsT=cma[8 + e_][:, tk],
                                                     rhs=cma[6 + e_][:, tk], start=True, stop=True),
                     reads=[cma[8 + e_].k, cma[6 + e_].k], writes=[pK.k])
            S.op("dve", lambda e: e.tensor_tensor(t2[:, :], DT[:, :], pB[:, 0:256], ALU.mult), reads=[DT.k, pB.k],
                 writes=[t2.k])
            S.op("dve", lambda e: e.tensor_tensor(HV(t3[:, :], 4, 64), HV(Dn[:, :], 4, 64),
                                                  smt[:, 12:16].unsqueeze(2).broadcast_to([128, 4, 64]), ALU.mult),
                 reads=[Dn.k, smt.k], writes=[t3.k])
            pK4 = pK[:, :].rearrange("p (e w l) -> p e w l", e=2, w=2)
            for hf in range(2):
                r = slice(64 * hf, 64 * hf + 64)
                kkb = pK4[r, :, 0, r].unsqueeze(2).broadcast_to([64, 2, 2, 64])
                kqb = pK4[r, :, 1, r].unsqueeze(2).broadcast_to([64, 2, 2, 64])
                e4 = lambda ap: ap.rearrange("p (e h) l -> p e h l", e=2)
                f4 = lambda ap: ap.rearrange("p (e h l) -> p e h l", e=2, h=2)
                S.op("dve", lambda e, r=r, kkb=kkb: e.tensor_tensor(e4(Qm[0][r, :, r]), f4(t2[r, :]), kkb, ALU.mult),
                     reads=[t2.k, pK.k], writes=[Qm[0].k])
                S.op("dve", lambda e, r=r, kkb=kkb: e.tensor_tensor(e4(Pm[0][r, :, r]), f4(t3[r, :]), kkb, ALU.mult),
                     reads=[t3.k, pK.k], writes=[Pm[0].k])
                S.op("dve", lambda e, r=r, kqb=kqb: e.tensor_tensor(e4(qkT[r, :, r]), f4(DT[r, :]), kqb, ALU.mult),
                     reads=[DT.k, pK.k], writes=[qkT.k])
                S.op("dve", lambda e, r=r: e.scalar_tensor_tensor(
                    Xm[0][r, :, r], Qm[0][r, :, r], -1.0,
                    consts[r, C_I2:C_I2 + 64].unsqueeze(1).broadcast_to([64, 4, 64]), ALU.mult, ALU.add),
                    reads=[Qm[0].k, consts.k], writes=[Xm[0].k])
            if ti == TAPT:
                tap('Q0', Qm[0][:, :, :], Qm[0].k, [128, 4, 128])
                tap('P0', Pm[0][:, :, :], Pm[0].k, [128, 4, 128])
                tap('qkT', qkT[:, :, :], qkT.k, [128, 4, 128])
            ck('a_x0')
            cur = 0
            for k in range(6):
                nxt = 1 - cur
                Qc, Pc, Xc, Qn, Pn, Xn = Qm[cur], Pm[cur], Xm[cur], Qm[nxt], Pm[nxt], Xm[nxt]
                pq = PS() if k < 5 else None
                px_ = PS() if k >= 1 else None
                pp = PS() if k < 4 else None
                if pq is not None:
                    for h in range(4):
                        S.op("pe", lambda e, h=h, Pc=Pc, Qc=Qc, pq=pq: e.matmul(
                            pq[:, h * 128:(h + 1) * 128], lhsT=Pc[:, h, :], rhs=Qc[:, h, :], start=True, stop=True),
                            reads=[Pc.k, Qc.k], writes=[pq.k])
                if pp is not None:
                    for h in range(4):
                        S.op("pe", lambda e, h=h, Pc=Pc, Qc=Qc, pp=pp: e.matmul(
                            pp[:, h * 128:(h + 1) * 128], lhsT=Qc[:, h, :], rhs=Pc[:, h, :], start=True, stop=True),
                            reads=[Pc.k, Qc.k], writes=[pp.k])
                if px_ is not None:
                    for h in range(4):
                        S.op("pe", lambda e, h=h, Pc=Pc, Xc=Xc, px_=px_: e.matmul(
                            px_[:, h * 128:(h + 1) * 128], lhsT=Pc[:, h, :], rhs=Xc[:, h, :], start=True, stop=True),
                            reads=[Pc.k, Xc.k], writes=[px_.k])
                if pq is not None:
                    S.op("act", lambda e, Qn=Qn, pq=pq: e.copy(Qn[:, :, :], HV(pq[:, :], 4, 128)), reads=[pq.k],
                         writes=[Qn.k])
                if pp is not None:
                    S.op("act", lambda e, Pn=Pn, pp=pp: e.copy(Pn[:, :, :], HV(pp[:, :], 4, 128)), reads=[pp.k],
                         writes=[Pn.k])
                elif k == 4:
                    pass
                if px_ is not None:
                    S.op("dve", lambda e, Xn=Xn, Xc=Xc, px_=px_: e.tensor_tensor(
                        Xn[:, :, :], Xc[:, :, :], HV(px_[:, :], 4, 128), ALU.add), reads=[Xc.k, px_.k], writes=[Xn.k])
                else:
                    S.op("dve", lambda e, Xn=Xn, Xc=Xc: e.tensor_copy(Xn[:, :, :], Xc[:, :, :]), reads=[Xc.k],
                         writes=[Xn.k])
                if k == 4:
                    pp5 = PS()
                    for h in range(4):
                        S.op("pe", lambda e, h=h, Pc=Pc, Qc=Qc: e.matmul(
                            pp5[:, h * 128:(h + 1) * 128], lhsT=Qc[:, h, :], rhs=Pc[:, h, :], start=True, stop=True),
                            reads=[Pc.k, Qc.k], writes=[pp5.k])
                    S.op("act", lambda e, Pn=Pn: e.copy(Pn[:, :, :], HV(pp5[:, :], 4, 128)), reads=[pp5.k],
                         writes=[Pn.k])
                cur = nxt
            Xf = Xm[cur]
            ck('a_neu')
            pU, pW = PS(), PS()
            for h in range(4):
                S.op("pe", lambda e, h=h: e.matmul(pU[:, h * 128:(h + 1) * 128], lhsT=Xf[:, h, :],
                                                   rhs=vb[:, h * 128:(h + 1) * 128], start=True, stop=True),
                     reads=[Xf.k, vb.k], writes=[pU.k])
                S.op("pe", lambda e, h=h: e.matmul(pW[:, h * 128:(h + 1) * 128], lhsT=kbg[:, h * 128:(h + 1) * 128],
                                                   rhs=Xf[:, h, :], start=True, stop=True),
                     reads=[Xf.k, kbg.k], writes=[pW.k])
            S.op("act", lambda e: e.copy(u_sb[:, :], pU[:, :]), reads=[pU.k], writes=[u_sb.k])
            S.op("act", lambda e: e.copy(wT[:, :, :], HV(pW[:, :], 4, 128)), reads=[pW.k], writes=[wT.k])
            if ti == TAPT:
                tap('Xf', Xf[:, :, :], Xf.k, [128, 4, 128])
                tap('u', u_sb[:, :], u_sb.k, [128, 512])
                tap('wT', wT[:, :, :], wT.k, [128, 4, 128])
            ck('a_uw')
            pO1 = [PS(), PS()]
            for c in range(2):
                r = slice(64 * c, 64 * c + 64)
                Sin, Sout = Sdn[(sp0 + c) % 2], Sdn[(sp0 + c + 1) % 2]
                pws = PS()
                for h in range(4):
                    hs = slice(h * 128, (h + 1) * 128)
                    S.op("pe", lambda e, h=h, hs=hs, Sin=Sin, pws=pws: e.matmul(pws[:, hs], lhsT=wT[:, h, :],
                                                                                rhs=Sin[:, hs], start=True, stop=True),
                         reads=[wT.k, Sin.k], writes=[pws.k])
                for h in range(4):
                    hs = slice(h * 128, (h + 1) * 128)
                    S.op("pe", lambda e, h=h, hs=hs, Sin=Sin, c=c: e.matmul(pO1[c][:, hs], lhsT=cma[6 + h // 2][:, tk],
                                                                            rhs=Sin[:, hs], start=True, stop=True),
                         reads=[cma[6 + h // 2].k, Sin.k], writes=[pO1[c].k])
                S.op("dve", lambda e, r=r, c=c, pws=pws: e.tensor_tensor(vnew2[c][r, :], u_sb[r, :], pws[r, :], ALU.subtract),
                     reads=[u_sb.k, pws.k], writes=[vnew2[c].k])
                pSn = PS()
                for h in range(4):
                    hs = slice(h * 128, (h + 1) * 128)
                    S.op("pe", lambda e, hs=hs, c=c, pSn=pSn: e.matmul(pSn[:, hs], lhsT=kdec[:, hs], rhs=vnew2[c][:, hs],
                                                                       start=True, stop=True),
                         reads=[kdec.k, vnew2[c].k], writes=[pSn.k])
                for h in range(4):
                    hs = slice(h * 128, (h + 1) * 128)
                    S.op("dve", lambda e, hs=hs, h=h, c=c, Sin=Sin, Sout=Sout, pSn=pSn: e.scalar_tensor_tensor(
                        Sout[:, hs], Sin[:, hs], cdc[c][:, 8 + h:9 + h], pSn[:, hs], ALU.mult, ALU.add),
                        reads=[Sin.k, cdc[c].k, pSn.k], writes=[Sout.k])
                S.op("dve", lambda e, r=r, c=c: e.tensor_tensor(
                    HV(o_sb[r, :], 4, 128), HV(pO1[c][r, :], 4, 128),
                    esm[r, 8:12].unsqueeze(2).broadcast_to([64, 4, 128]), ALU.mult),
                    reads=[pO1[c].k, esm.k], writes=[o_sb.k])
            s_par[0] = sp0
            pO2 = PS()
            for h in range(4):
                hs = slice(h * 128, (h + 1) * 128)
                for c in range(2):
                    S.op("pe", lambda e, h=h, hs=hs, c=c: e.matmul(pO2[:, hs], lhsT=qkT[:, h, :], rhs=vnew2[c][:, hs],
                                                                   start=(c == 0), stop=(c == 1)),
                         reads=[qkT.k, vnew2[c].k], writes=[pO2.k])
            S.op("dve", lambda e: e.tensor_tensor(o_sb[:, :], o_sb[:, :], pO2[:, :], ALU.add), reads=[o_sb.k, pO2.k],
                 writes=[o_sb.k])
            if ti == TAPT:
                tap('vnew0', vnew2[0][:, :], vnew2[0].k, [128, 512])
                tap('vnew1', vnew2[1][:, :], vnew2[1].k, [128, 512])
                tap('o_pre', o_sb[:, :], o_sb.k, [128, 512])
                tap('Sdn', Sdn[sp0][:, :], Sdn[sp0].k, [128, 512])
            ck('a_rec')
            yield
            S.op("dve", lambda e: e.tensor_tensor(t1[:, :], o_sb[:, :], o_sb[:, :], ALU.mult), reads=[o_sb.k],
                 writes=[t1.k])
            S.op("dve", lambda e: e.tensor_reduce(st2[:, 4:8], HV(t1[:, :], 4, 128), AX.X, ALU.add), reads=[t1.k],
                 writes=[st2.k])
            S.op("act", lambda e: e.activation(st2[:, 8:12], st2[:, 4:8], AF.Ln, bias=EPS, scale=1.0 / 128),
                 reads=[st2.k], writes=[st2.k])
            S.op("act", lambda e: e.activation(st2[:, 12:16], st2[:, 8:12], AF.Exp, scale=-0.5), reads=[st2.k],
                 writes=[st2.k])
            S.op("dve", lambda e: e.tensor_tensor(HV(o_sb[:, :], 4, 128), HV(o_sb[:, :], 4, 128),
                                                  st2[:, 12:16].unsqueeze(2).broadcast_to([128, 4, 128]), ALU.mult),
                 reads=[o_sb.k, st2.k], writes=[o_sb.k])
            S.op("dve", lambda e: e.tensor_tensor(HV(o_sb[:, :], 4, 128), HV(o_sb[:, :], 4, 128),
                                                  rowp[:, 544:672].unsqueeze(1).broadcast_to([128, 4, 128]), ALU.mult),
                 reads=[o_sb.k, rowp.k], writes=[o_sb.k])
            S.op("dve", lambda e: e.tensor_tensor(yb[:, 512:1024], o_sb[:, :], zd[par][:, :], ALU.mult),
                 reads=[o_sb.k, zd[par].k], writes=[yb.k])
            if ti == TAPT:
                tap('yd', yb[:, 512:1024], yb.k, [128, 512], BF16)
            if u_idx is not None:
                pt = PS()
                ptb = pt.t[:, :].bitcast(BF16)
                for i in range(8):
                    S.op("pe", lambda e, i=i: e.transpose(ptb[:, i * 128:(i + 1) * 128], yb[:, i * 128:(i + 1) * 128],
                                                          ident_b[:, :]), reads=[yb.k, ident_b.k], writes=[pt.k])
                ys_ = yT[u_idx % 2]
                S.op("act", lambda e: e.copy(ys_[:, :, ucol:ucol + 128], ptb.rearrange("p (b t) -> p b t", b=8)),
                     reads=[pt.k], writes=[ys_.k])

        def phase_a_super(tiles):
            n = len(tiles)
            N = 128 * n
            for i, ti in enumerate(tiles):
                xt = xt2[ti % 2]
                if ti == 0 and not os.environ.get('FULL0'):
                    S.op("dve", lambda e: e.memset(xt[:, :], 0.0), writes=[xt.k])
                    S.dma("sp", lambda e: e.dma_start(out=xt[112:128, :], in_=xin[112:128, :]), "x%d" % (ti % 2),
                          writes=[xt.k])
                else:
                    S.dma("sp", lambda e, ti=ti, xt=xt: e.dma_start(out=xt[:, :], in_=xin[ti * 128:(ti + 1) * 128, :]),
                          "x%d" % (ti % 2), writes=[xt.k])
                norm_transpose(xt[:, :], xt.k, xn, sq, st, xnT, xnT.k, i * 128, 0)
            ck('a_norm')
            for blk in range(14):
                p = PS()
                for k in range(8):
                    S.op("pe", lambda e, k=k, p=p, blk=blk: e.matmul(p[:, 0:N], lhsT=wcm[:, k, blk * 128:(blk + 1) * 128],
                                                                     rhs=xnT[:, k, 0:N], start=(k == 0), stop=(k == 7)),
                         reads=[wcm.k, xnT.k], writes=[p.k])
                pr, ac = pre[blk % 2], acc[blk % 2]
                S.op("act", lambda e, pr=pr, blk=blk: e.copy(pr[:, 0:3], halo[:, blk, :]), reads=[halo.k], writes=[pr.k])
                S.op("act", lambda e, pr=pr, p=p: e.copy(pr[:, 3:3 + N], p[:, 0:N]), reads=[p.k], writes=[pr.k])
                S.op("act", lambda e, pr=pr, blk=blk: e.copy(halo[:, blk, :], pr[:, N:N + 3]), reads=[pr.k],
                     writes=[halo.k])
                S.op("act", lambda e, pr=pr, ac=ac, blk=blk: e.activation(ac[:, 0:N], pr[:, 0:N], AF.Identity,
                                                                          scale=cw[:, blk, 0:1]),
                     reads=[pr.k, cw.k], writes=[ac.k])
                for j in range(1, 4):
                    S.op("dve", lambda e, pr=pr, ac=ac, blk=blk, j=j: e.scalar_tensor_tensor(
                        ac[:, 0:N], pr[:, j:j + N], cw[:, blk, j:j + 1], ac[:, 0:N], ALU.mult, ALU.add),
                        reads=[pr.k, cw.k, ac.k], writes=[ac.k])
                S.op("act", lambda e, ac=ac, blk=blk: e.activation(cma[blk][:, 0:N], ac[:, 0:N], AF.Silu,
                                                                   bias=cw[:, blk, 4:5], scale=1.0),
                     reads=[ac.k, cw.k], writes=[cma[blk].k])
            ck('a_cm')
            for blk in (6, 7, 8, 9):
                S.op("dve", lambda e, blk=blk: e.tensor_tensor(acc[0][:, 0:N], cma[blk][:, 0:N], cma[blk][:, 0:N], ALU.mult),
                     reads=[cma[blk].k], writes=[acc[0].k])
                p = PS()
                S.op("pe", lambda e, p=p: e.matmul(p[:, 0:N], lhsT=ones_f[:, :], rhs=acc[0][:, 0:N], start=True,
                                                   stop=True), reads=[ones_f.k, acc[0].k], writes=[p.k])
                S.op("act", lambda e, p=p: e.activation(rs[:, 0:N], p[:, 0:N], AF.Ln, bias=EPS, scale=1.0),
                     reads=[p.k], writes=[rs.k])
                S.op("act", lambda e: e.activation(rs[:, 0:N], rs[:, 0:N], AF.Exp, scale=-0.5), reads=[rs.k],
                     writes=[rs.k])
                sc = (128.0 ** -0.5) if blk < 8 else 1.0
                S.op("dve", lambda e, blk=blk, sc=sc: e.scalar_tensor_tensor(
                    cma[blk][:, 0:N], cma[blk][:, 0:N], sc, rs[:, 0:N], ALU.mult, ALU.mult),
                    reads=[cma[blk].k, rs.k], writes=[cma[blk].k])
            if tiles[0] == TAPT:
                tap('xnT', xnT[:, :, 0:128], xnT.k, [128, 8, 128], BF16)
                for b_ in range(14):
                    tap('cma%d' % b_, cma[b_][:, 0:128], cma[b_].k, [128, 128])
            ck('a_l2')
            heads = [phase_a_tile(ti, None, i * 128) for i, ti in enumerate(tiles)]
            next(heads[0])
            for i, ti in enumerate(tiles):
                for _ in heads[i]:
                    pass
                ck('a_t1')
                if ti == 0:
                    u_idx, ucol = None, 0
                else:
                    u_idx, ucol = (ti - 1) // 4, ((ti - 1) % 4) * 128
                g2 = phase_a_tile2(ti, i * 128, u_idx, ucol)
                next(g2)
                if i + 1 < n:
                    next(heads[i + 1])
                for _ in g2:
                    pass
                if ti >= 1 and (ti - 1) % 4 == 3:
                    u = (ti - 1) // 4
                    ys_ = yT[u % 2]
                    srcap = ysrc[u].ap()
                    dstap = yround[u // 4].ap()[(u % 4) * 4096:(u % 4 + 1) * 4096, :]
                    utk = Trk()
                    S.dma("sp", lambda e, ys_=ys_, srcap=srcap: e.dma_start(
                        out=srcap.rearrange("(b p) t -> p b t", p=128), in_=ys_[:, :, :]), "ys",
                        reads=[ys_.k], writes=[utk])
                    S.dma("pool", lambda e, srcap=srcap, dstap=dstap: e.collective_compute(
                        "AllGather", ALU.bypass, replica_groups=GROUPS, ins=[srcap.opt()], outs=[dstap.opt()]),
                        "ag", reads=[utk], writes=[ag_trk[u]], inc=1)

        ag_trk = [Trk() for _ in range(NU)]
        ck('setup')
        if not os.environ.get('SKIP0'):
            phase_a_super([0])
        ck('meta')
        if stop == 'meta#2':
            phase_a_super([0])
            ck('meta')
        for s_ in range(NU):
            phase_a_super([1 + 4 * s_ + j_ for j_ in range(4)])
            ck('super%d' % s_)

        ck('phaseA')
        while len(stack) > mark_a:
            stack.pop().__exit__(None, None, None)
        new_epoch()

        qv = {}

        def Q(e):
            if "q" not in qv:
                qv["q"] = e.partition_id() % 4
            return qv["q"]

        def XW(e):
            if "xw" not in qv:
                qv["xw"] = xin[bass.ds(Q(e) * 512, TOKA - 1536), :]
            return qv["xw"]

        def YW(e, bi):
            if ("yw", bi) not in qv:
                if "q4" not in qv:
                    qv["q4"] = Q(e) * 4096
                qv[("yw", bi)] = yround[bi].ap()[bass.ds(qv["q4"], 4096), :]
            return qv[("yw", bi)]

        NT_B = 4 * NBLK
        h1 = sb("h1", [128, NT_B, D])
        h1_k = [Trk() for _ in range(NT_B)]
        xn2T = sb("xn2T", [128, 8, 512 * NBLK], BF16)
        xn2_k = [Trk() for _ in range(NBLK)]
        xnb = sb("xnb", [128, D], BF16)
        sqb = sb("sqb", [128, D], BF16)
        stb = sb("stb", [128, 4])
        mark_mix = len(stack)
        wout = sb("wout", [128, 8, D], BF16)
        for k in range(8):
            S.dma("pool", lambda e, k=k: e.dma_start(out=wout[:, k, :], in_=wout_d[k * 128:(k + 1) * 128, :]),
                  "wo%d" % (k % 2), writes=[wout.k])
        for kk_ in range(2):
            wout.k.w["d_wo%d" % kk_] = S.dma_cnt["d_wo%d" % kk_]
        wgmP = [sb("wgm%d" % i, [128, 8, 2, 128], BF16) for i in range(2)]
        wbrmP = [sb("wbrm%d" % i, [128, 32, 128], BF16) for i in range(2)]
        xT = sb("xTb", [128, 8, 512], BF16)
        yTb = sb("yTb", [128, 32, 512], BF16)
        g_sb = [sb("g%d" % i, [128, 512]) for i in range(2)]
        mt = sb("mt", [128, 512])
        mrgT = sb("mrgT", [128, 8, 512], BF16)

        for bi in range(NBLK):
            row0 = 128 + 2048 * bi
            for i in range(4):
                t = 4 * bi + i
                S.dma("sp", lambda e, t=t, i=i, row0=row0: e.dma_start(
                    out=h1[:, t, :], in_=XW(e)[row0 + i * 128:row0 + (i + 1) * 128, :]), "xb%d" % (i % 2),
                    writes=[h1_k[t]])
                norm_transpose(h1[:, t, :], h1_k[t], xnb, sqb, stb, xT, xT.k, i * 128, 0)
            yr = yround[bi].ap()
            for c4 in range(4):
                S.dma("sp", lambda e, c4=c4, bi=bi: e.dma_start(
                        out=yTb[:, c4 * 8:(c4 + 1) * 8, :],
                        in_=YW(e, bi)[c4 * 1024:(c4 + 1) * 1024, :].rearrange("(c p) t -> p c t", p=128)),
                        "yl%d" % c4, reads=[ag_trk[4 * bi + j] for j in range(4)], writes=[yTb.k])
            if bi == 0:
                tap('b_xT', xT[:, :, :], xT.k, [128, 8, 512], BF16)
                tap('b_yT', yTb[:, :, :], yTb.k, [128, 32, 512], BF16)
            def load_wm(m):
                wgm_, wbrm_ = wgmP[m % 2], wbrmP[m % 2]
                S.dma("pool", lambda e, m=m, wgm_=wgm_: e.dma_start(out=wgm_[:, :, :, :].rearrange("p k n j -> p (k n j)"),
                                                                    in_=wgm_d[m, :, :]), "wm0_%d" % (m % 2), writes=[wgm_.k])
                S.dma("pool", lambda e, m=m, wbrm_=wbrm_: e.dma_start(out=wbrm_[:, :, :].rearrange("p c j -> p (c j)"),
                                                                      in_=wbrm_d[m, :, :]), "wm1_%d" % (m % 2), writes=[wbrm_.k])
            load_wm(0)
            for m in range(8):
                if m + 1 < 8:
                    load_wm(m + 1)
                wgm, wbrm = wgmP[m % 2], wbrmP[m % 2]
                pbs = []
                for n in range(2):
                    pg = PS()
                    for k in range(8):
                        S.op("pe", lambda e, k=k, n=n, pg=pg, wgm=wgm: e.matmul(pg[:, :], lhsT=wgm[:, k, n, :], rhs=xT[:, k, :],
                                                                       start=(k == 0), stop=(k == 7)),
                             reads=[wgm.k, xT.k], writes=[pg.k])
                    S.op("act", lambda e, n=n, pg=pg: e.activation(g_sb[n][:, :], pg[:, :], AF.Sigmoid), reads=[pg.k],
                         writes=[g_sb[n].k])
                    pb = PS()
                    for kk in range(16):
                        cidx = (kk // 4) * 8 + n * 4 + (kk % 4)
                        S.op("pe", lambda e, kk=kk, n=n, pb=pb, cidx=cidx, wbrm=wbrm: e.matmul(
                            pb[:, :], lhsT=wbrm[:, n * 16 + kk, :], rhs=yTb[:, cidx, :], start=(kk == 0),
                            stop=(kk == 15)), reads=[wbrm.k, yTb.k], writes=[pb.k])
                    pbs.append(pb)
                if bi == 0 and m == 0:
                    tap('b_g0', g_sb[0][:, :], g_sb[0].k, [128, 512])
                    tap('b_g1', g_sb[1][:, :], g_sb[1].k, [128, 512])
                S.op("dve", lambda e, pb=pbs[0]: e.tensor_tensor(mt[:, :], g_sb[0][:, :], pb[:, :], ALU.mult),
                     reads=[g_sb[0].k, pbs[0].k], writes=[mt.k])
                S.op("dve", lambda e, pb=pbs[1]: e.tensor_tensor(g_sb[1][:, :], g_sb[1][:, :], pb[:, :], ALU.mult),
                     reads=[g_sb[1].k, pbs[1].k], writes=[g_sb[1].k])
                S.op("dve", lambda e, m=m: e.tensor_tensor(mrgT[:, m, :], mt[:, :], g_sb[1][:, :], ALU.add),
                     reads=[mt.k, g_sb[1].k], writes=[mrgT.k])
            if bi == 0:
                tap('b_mrg', mrgT[:, :, :], mrgT.k, [128, 8, 512], BF16)
            for i in range(4):
                t = 4 * bi + i
                for hf in range(2):
                    p = PS()
                    for k in range(8):
                        S.op("pe", lambda e, k=k, i=i, hf=hf, p=p: e.matmul(
                            p[:, :], lhsT=mrgT[:, k, i * 128:(i + 1) * 128], rhs=wout[:, k, hf * 512:(hf + 1) * 512],
                            start=(k == 0), stop=(k == 7)), reads=[mrgT.k, wout.k], writes=[p.k])
                    S.op("dve", lambda e, t=t, hf=hf, p=p: e.tensor_tensor(
                        h1[:, t, hf * 512:(hf + 1) * 512], h1[:, t, hf * 512:(hf + 1) * 512], p[:, :], ALU.add),
                        reads=[p.k, h1_k[t]], writes=[h1_k[t]])
                if t == 0:
                    tap('b_h1', h1[:, 0, :], h1_k[0], [128, 1024])
                norm_transpose(h1[:, t, :], h1_k[t], xnb, sqb, stb, xn2T, xn2_k[bi], bi * 512 + i * 128, 1)

        while len(stack) > mark_mix:
            stack.pop().__exit__(None, None, None)
        new_epoch()

        ck('mix')
        wguq_fP = [sb("wguq%d" % i, [128, 8 * 2 * 6 * 128], BF16) for i in range(2)]
        wdnq_fP = [sb("wdnq%d" % i, [128, 6 * D], BF16) for i in range(2)]
        actT = sb("actT", [128, 6, 512], BF16)
        gs = sb("gs", [128, 512])
        ob = sb("ob", [128, D])

        def load_q(qi):
            b0, nb = QB[qi]
            w = nb * 128
            wg_, wd_ = wguq_fP[qi % 2], wdnq_fP[qi % 2]
            k0, k1 = "fq%d_0" % (qi % 2), "fq%d_1" % (qi % 2)
            for kn in range(16):
                S.dma("pool", lambda e, qi=qi, w=w, kn=kn, wg_=wg_: e.dma_start(
                    out=wg_[:, kn * w:(kn + 1) * w], in_=wguq_d[qi][:, kn * w:(kn + 1) * w]), (k0, k1)[kn % 2],
                    writes=[wg_.k])
            for c_ in range(nb):
                S.dma("pool", lambda e, qi=qi, c_=c_, wd_=wd_: e.dma_start(
                    out=wd_[:, c_ * D:(c_ + 1) * D], in_=wdnq_d[qi][:, c_ * D:(c_ + 1) * D]), (k0, k1)[c_ % 2],
                    writes=[wd_.k])
            for kk_ in (k0, k1):
                wg_.k.w["d_" + kk_] = S.dma_cnt["d_" + kk_]
                wd_.k.w["d_" + kk_] = S.dma_cnt["d_" + kk_]

        load_q(0)
        for qi, (b0, nb) in enumerate(QB):
            w = nb * 128
            if qi + 1 < 4:
                load_q(qi + 1)
            wguq_f, wdnq_f = wguq_fP[qi % 2], wdnq_fP[qi % 2]
            wguq = wguq_f[:, 0:16 * w].rearrange("p (k n j) -> p k n j", k=8, n=2)
            wdnq = wdnq_f[:, 0:nb * D].rearrange("p (c d) -> p c d", c=nb)
            for bi in range(NBLK):
                tk = slice(bi * 512, (bi + 1) * 512)
                for j in range(nb):
                    pg, pu = PS(), PS()
                    for (pp, n) in ((pg, 0), (pu, 1)):
                        for k in range(8):
                            S.op("pe", lambda e, k=k, n=n, j=j, pp=pp, tk=tk, wguq=wguq: e.matmul(
                                pp[:, :], lhsT=wguq[:, k, n, j * 128:(j + 1) * 128], rhs=xn2T[:, k, tk], start=(k == 0),
                                stop=(k == 7)), reads=[wguq_f.k, xn2_k[bi]], writes=[pp.k])
                    S.op("act", lambda e, pg=pg: e.activation(gs[:, :], pg[:, :], AF.Silu), reads=[pg.k], writes=[gs.k])
                    S.op("dve", lambda e, j=j, pu=pu: e.tensor_tensor(actT[:, j, :], gs[:, :], pu[:, :], ALU.mult),
                         reads=[gs.k, pu.k], writes=[actT.k])
                for i in range(4):
                    t = 4 * bi + i
                    for hf in range(2):
                        p = PS()
                        for j in range(nb):
                            S.op("pe", lambda e, j=j, i=i, hf=hf, p=p, nb=nb, wdnq=wdnq: e.matmul(
                                p[:, :], lhsT=actT[:, j, i * 128:(i + 1) * 128], rhs=wdnq[:, j, hf * 512:(hf + 1) * 512],
                                start=(j == 0), stop=(j == nb - 1)), reads=[actT.k, wdnq_f.k], writes=[p.k])
                        S.op("dve", lambda e, t=t, hf=hf, p=p: e.tensor_tensor(
                            h1[:, t, hf * 512:(hf + 1) * 512], h1[:, t, hf * 512:(hf + 1) * 512], p[:, :], ALU.add),
                            reads=[p.k, h1_k[t]], writes=[h1_k[t]])
        tap('b_h2', h1[:, 0, :], h1_k[0], [128, 1024])
        outs = []
        for t in range(NT_B):
            S.op("act", lambda e, t=t: e.activation(sqb[:, :], h1[:, t, :], AF.Square, accum_out=stb[:, 0:1]),
                 reads=[h1_k[t]], writes=[sqb.k, stb.k])
            S.op("act", lambda e: e.activation(stb[:, 1:2], stb[:, 0:1], AF.Sqrt, bias=EPS, scale=1.0 / D),
                 reads=[stb.k], writes=[stb.k])
            S.op("dve", lambda e: e.reciprocal(stb[:, 2:3], stb[:, 1:2]), reads=[stb.k], writes=[stb.k])
            S.op("dve", lambda e, t=t: e.scalar_tensor_tensor(ob[:, :], h1[:, t, :], stb[:, 2:3], nwrow[:, :], ALU.mult,
                                                              ALU.mult), reads=[h1_k[t], stb.k, nwrow.k], writes=[ob.k])
            outs.append(S.dma("sp", lambda e, t=t: e.dma_start(out=out_d[t * 128:(t + 1) * 128, :], in_=ob[:, :]),
                              "o%d" % (t % 2), reads=[ob.k]))
        S.final_wait("sp", outs[-2:] if len(outs) >= 2 else outs)
    except _Stop:
        pass
    if taps:
        S.final_wait("sp", list(taps.values()))
    build.last_sched = S
    S.emit_all()
    while stack:
        stack.pop().__exit__(None, None, None)
    S.close()
    return nc


_CACHE = {}


def _prep_core(b, g, x, meta_tokens, p):
    w_in = p["w_in"][0]
    sl = lambda o, n: w_in[:, o:o + n]
    wcm = np.concatenate([sl(2048 + 512 * g, 512), sl(4096 + 128 * g, 128), sl(4608 + 128 * g, 128),
                          sl(5152 + 256 * g, 256), sl(6176 + 256 * g, 256), sl(7200 + 512 * g, 512)], axis=1)
    wtm = np.concatenate([sl(512 * g, 512), sl(9280 + 512 * g, 512), sl(5120 + 8 * g, 8), sl(9248 + 4 * g, 4),
                          sl(9264 + 4 * g, 4)], axis=1)
    scw, scb, dcw = p["ssd_conv_w"][0], p["ssd_conv_b"][0], p["dn_conv_w"][0]
    cwl, cbl = [], []
    for (src, o, n, bias) in ((scw, 512 * g, 512, scb), (scw, 2048 + 128 * g, 128, scb),
                              (scw, 2560 + 128 * g, 128, scb), (dcw, 256 * g, 256, None),
                              (dcw, 1024 + 256 * g, 256, None), (dcw, 2048 + 512 * g, 512, None)):
        cwl.append(src[:, o:o + n].T)
        cbl.append(bias[o:o + n] if bias is not None else np.zeros(n, np.float32))
    cw = np.concatenate([np.concatenate(cwl, 0), np.concatenate(cbl)[:, None]], axis=1)
    cw = cw.reshape(14, 128, 5).transpose(1, 0, 2).reshape(128, 70)
    rowp = np.concatenate([p["ssd_dt_bias"][0][8 * g:8 * g + 8], p["dn_dt_bias"][0][4 * g:4 * g + 4],
                           p["ssd_a_log"][0][8 * g:8 * g + 8], p["dn_a_log"][0][4 * g:4 * g + 4],
                           p["ssd_d"][0][8 * g:8 * g + 8], p["ssd_norm_w"][0][512 * g:512 * g + 512],
                           p["dn_norm_w"][0]])[None, :]
    xin = np.concatenate([np.zeros((112, D), np.float32), meta_tokens, x[b]], axis=0)
    return {"xin": np.ascontiguousarray(xin, np.float32), "wcm": np.ascontiguousarray(wcm, np.float32),
            "wtm": np.ascontiguousarray(wtm, np.float32), "cw": np.ascontiguousarray(cw, np.float32),
            "rowp": np.ascontiguousarray(rowp, np.float32)}


def kernel(x, meta_tokens, mix_norm_w, w_in, ssd_conv_w, ssd_conv_b, ssd_dt_bias, ssd_a_log, ssd_d, ssd_norm_w,
           dn_conv_w, dn_dt_bias, dn_a_log, dn_norm_w, w_branch, w_out, ffn_norm_w, w_gate_up, w_down,
           final_norm_w):
    p = dict(w_in=w_in, ssd_conv_w=ssd_conv_w, ssd_conv_b=ssd_conv_b, ssd_dt_bias=ssd_dt_bias, ssd_a_log=ssd_a_log,
             ssd_d=ssd_d, ssd_norm_w=ssd_norm_w, dn_conv_w=dn_conv_w, dn_dt_bias=dn_dt_bias, dn_a_log=dn_a_log,
             dn_norm_w=dn_norm_w)
    p = {k: np.asarray(v, np.float32) for k, v in p.items()}
    x = np.asarray(x, np.float32)
    meta_tokens = np.asarray(meta_tokens, np.float32)
    B, L, _ = x.shape
    assert B == 2 and L % 2048 == 0
    NU = L // 512
    NBLK = NU // 4
    if NU not in _CACHE:
        _CACHE[NU] = build(NU)
    nc = _CACHE[NU]
    w_in0 = p["w_in"][0]
    wg = w_in0[:, 11328:13376].reshape(8, 128, 2, 8, 128)
    wgm = np.ascontiguousarray(wg.transpose(3, 1, 0, 2, 4)).reshape(8, 128, 8 * 2 * 128)
    wb = np.asarray(w_branch, np.float32)[0].reshape(32, 128, 8, 128)
    wbrm = np.ascontiguousarray(wb.transpose(2, 1, 0, 3)).reshape(8, 128, 32 * 128)
    wgu = np.asarray(w_gate_up, np.float32)[0].reshape(8, 128, 2, 22, 128)
    wdn = np.asarray(w_down, np.float32)[0].reshape(22, 128, D)
    nw3 = np.stack([np.asarray(mix_norm_w, np.float32)[0], np.asarray(ffn_norm_w, np.float32)[0],
                    np.asarray(final_norm_w, np.float32)])
    shared = {"nw": np.ascontiguousarray(nw3.reshape(3, 8, 128).transpose(2, 0, 1)).reshape(128, 24),
              "nwr": np.ascontiguousarray(np.asarray(final_norm_w, np.float32)[None, :]),
              "consts": make_consts(), "wgm": wgm, "wbrm": wbrm,
              "wout": np.ascontiguousarray(np.asarray(w_out, np.float32)[0])}
    for qi, (b0, nb) in enumerate([(0, 6), (6, 6), (12, 5), (17, 5)]):
        shared["wguq%d" % qi] = np.ascontiguousarray(
            wgu[:, :, :, b0:b0 + nb, :].transpose(1, 0, 2, 3, 4)).reshape(128, 8 * 2 * nb * 128)
        shared["wdnq%d" % qi] = np.ascontiguousarray(wdn[b0:b0 + nb].transpose(1, 0, 2)).reshape(128, nb * D)
    in_maps = []
    for c in range(8):
        m = _prep_core(c // 4, c % 4, x, meta_tokens, p)
        m.update(shared)
        in_maps.append(m)
    res = run_bass_kernel_spmd(nc, in_maps, core_ids=list(range(8)))
    kernel.last_res = res
    out = np.zeros((B, L, D), np.float32)
    for c in range(8):
        b, q = c // 4, c % 4
        o = np.asarray(res.results[c]["out"]).reshape(NBLK, 512, D)
        for bi in range(NBLK):
            u = 4 * bi + q
            out[b, 512 * u:512 * (u + 1)] = o[bi]
    return out
```

```python
import os
import numpy as np
import concourse.bass as bass
import concourse.mybir as mybir
from concourse.bass_utils import run_bass_kernel_spmd

F32 = mybir.dt.float32
BF16 = mybir.dt.bfloat16
AF = mybir.ActivationFunctionType
ALU = mybir.AluOpType
AX = mybir.AxisListType

D = 1024
EPS = 1e-6
NEG = -30000.0
DFF = 2816
NCM = 1792
NTM = 1040
NCONST = 128 * 6 + 64 * 2 + 512 + 256 + 256
GROUPS = [[0, 1, 2, 3], [4, 5, 6, 7]]


class Trk:
    __slots__ = ("w", "r", "ps")

    def __init__(self, ps=False):
        self.w = {}
        self.r = {}
        self.ps = ps


class Sched:
    def __init__(self, nc):
        self.nc = nc
        self.prog = {k: [] for k in ("pe", "dve", "act", "pool", "sp")}
        self.cnt = {k: 0 for k in self.prog}
        self.seen = {k: {} for k in self.prog}
        self.sems = {}
        self.dma_cnt = {}
        self._cms = []
        self.log = {k: [] for k in self.prog}
        self.wkeys = {}

    def sem(self, key):
        if key not in self.sems:
            cm = self.nc.semaphore("s_" + key)
            self.sems[key] = cm.__enter__()
            self._cms.append(cm)
        return self.sems[key]

    def _waits(self, eng, reads, writes):
        self._lastw = []
        need = {}

        def add(k, v, raw=False):
            if (k != eng or (raw and eng != "pe")) and need.get(k, 0) < v:
                need[k] = v
        for t in reads:
            for k, v in t.w.items():
                add(k, v, True)
            if t.ps:
                for k, v in t.r.items():
                    add(k, v)
        for t in writes:
            for k, v in t.w.items():
                add(k, v)
            for k, v in t.r.items():
                add(k, v)
        out = []
        for k, v in need.items():
            if self.seen[eng].get(k, 0) < v:
                self.seen[eng][k] = v
                out.append((self.sem(k if k.startswith("d_") else "e_" + k), v))
                self._lastw.append((k if k.startswith("d_") else "e_" + k, v))
        return out

    def _mark(self, key, v, reads, writes):
        for t in reads:
            if t.r.get(key, 0) < v:
                t.r[key] = v
        for t in writes:
            t.w[key] = v
            t.r = {}

    def op(self, eng, fn, reads=(), writes=()):
        wl = self._waits(eng, reads, writes)
        self.cnt[eng] += 1
        semh = self.sem("e_" + eng)

        def emit(e, fn=fn, wl=wl, semh=semh):
            for s, v in wl:
                e.wait_ge(s, v)
            fn(e).then_inc(semh, 1)
        self.prog[eng].append(emit)
        self.log[eng].append((list(self._lastw), ("e_" + eng, 1)))
        self._mark(eng, self.cnt[eng], reads, writes)

    def dma(self, eng, fn, semkey, reads=(), writes=(), inc=16):
        key = "d_" + semkey
        wl = self._waits(eng, reads, writes)
        self.dma_cnt[key] = self.dma_cnt.get(key, 0) + inc
        v = self.dma_cnt[key]
        semh = self.sem(key)

        def emit(e, fn=fn, wl=wl, semh=semh, inc=inc):
            for s, vv in wl:
                e.wait_ge(s, vv)
            if inc == 1:
                fn(e).then_inc(semh)
            else:
                fn(e).then_inc(semh, inc)
        self.prog[eng].append(emit)
        self.log[eng].append((list(self._lastw), (key, inc)))
        self._mark(key, v, reads, writes)
        return (key, v)

    def final_wait(self, eng, deps):
        wl = [(self.sem(k), v) for k, v in deps]

        def emit(e, wl=wl):
            for s, v in wl:
                e.wait_ge(s, v)
        self.prog[eng].append(emit)

    def emit_all(self):
        nc = self.nc
        with nc.Block() as block:
            @block.tensor
            def _(e):
                for f in self.prog["pe"]:
                    f(e)

            @block.vector
            def _(e):
                for f in self.prog["dve"]:
                    f(e)

            @block.scalar
            def _(e):
                for f in self.prog["act"]:
                    f(e)

            @block.gpsimd
            def _(e):
                for f in self.prog["pool"]:
                    f(e)

            @block.sync
            def _(e):
                for f in self.prog["sp"]:
                    f(e)

    def close(self):
        for cm in reversed(self._cms):
            cm.__exit__(None, None, None)


class Buf:
    def __init__(self, t):
        self.t = t
        self.k = Trk()

    def __getitem__(self, idx):
        return self.t[idx]


def make_consts():
    c = np.zeros((128, NCONST), np.float32)
    t = np.arange(128)
    tc, tp = t // 64, t % 64
    same = (tc[:, None] == tc[None, :]).astype(np.float32)
    o = 0
    c[:, o:o + 128] = np.eye(128); o += 128
    c[:, o:o + 128] = same; o += 128
    c[:, o:o + 128] = -same * (tp[:, None] <= tp[None, :]); o += 128
    c[:, o:o + 128] = same * (tp[:, None] <= tp[None, :]); o += 128
    c[:, o:o + 128] = -same; o += 128
    c[:, o:o + 128] = same * (tp[:, None] > tp[None, :]); o += 128
    l = np.arange(64)
    c[:, o:o + 64] = (tp[:, None] <= l[None, :]); o += 64
    c[:, o:o + 64] = (tp[:, None] == l[None, :]); o += 64
    mA = NEG * (tp[:, None] > l[None, :]).astype(np.float32)
    c[:, o:o + 512] = np.tile(mA, (1, 8)); o += 512
    mN = NEG * (l[None, :] >= tp[:, None]).astype(np.float32)
    c[:, o:o + 256] = np.tile(mN, (1, 4)); o += 256
    c[0:64, o:o + 128] = 1.0; o += 128
    c[64:128, o:o + 128] = 1.0; o += 128
    assert o == NCONST
    return c


C_ID, C_BD, C_NLE, C_LE, C_NBD, C_GT = [128 * i for i in range(6)]
C_U2 = 768
C_I2 = 832
C_MA = 896
C_MN = 1408
C_C0 = 1664


class _Stop(Exception):
    pass


def build(NU, stop=None):
    ckc = {}

    def ck(name):
        ckc[name] = ckc.get(name, 0) + 1
        if stop == name or stop == '%s#%d' % (name, ckc[name]):
            raise _Stop()
    NTA = 1 + 4 * NU
    TOKA = 128 * NTA
    NBLK = NU // 4
    nc = bass.Bass("TRN2", target_bir_lowering=False)
    dt_in = lambda n, s: nc.dram_tensor(n, s, F32, kind="ExternalInput").ap()
    xin = dt_in("xin", [TOKA, D])
    wcm_d = dt_in("wcm", [D, NCM])
    wtm_d = dt_in("wtm", [D, NTM])
    cw_d = dt_in("cw", [128, 14 * 5])
    nw_d = dt_in("nw", [128, 24])
    nwr_d = dt_in("nwr", [1, D])
    rowp_d = dt_in("rowp", [1, 672])
    const_d = dt_in("consts", [128, NCONST])
    QB = [(0, 6), (6, 6), (12, 5), (17, 5)]
    if stop is None or stop in ('mix', 'phaseA'):
        wgm_d = dt_in("wgm", [8, 128, 8 * 2 * 128])
        wbrm_d = dt_in("wbrm", [8, 128, 32 * 128])
        wout_d = dt_in("wout", [D, D])
        wguq_d = [dt_in("wguq%d" % i, [128, 8 * 2 * nb * 128]) for i, (b0, nb) in enumerate(QB)]
        wdnq_d = [dt_in("wdnq%d" % i, [128, nb * D]) for i, (b0, nb) in enumerate(QB)]
    out_d = nc.dram_tensor("out", [NBLK * 512, D], F32, kind="ExternalOutput").ap()
    ysrc = [nc.dram_tensor("ysrc%d" % u, [1024, 512], BF16) for u in range(NU)]
    yround = [nc.dram_tensor("yround%d" % r, [4 * 4096, 512], BF16) for r in range(NBLK)]

    S = Sched(nc)
    stack = []
    DBG = bool(os.environ.get("KDBG"))
    TAPT = int(os.environ.get("TAPT", "1"))
    taps = {}

    def tap(name, ap, trk, shape, dt=F32):
        if not DBG or name in taps:
            return
        d = nc.dram_tensor("dbg_" + name, shape, dt, kind="ExternalOutput").ap()
        taps[name] = S.dma("sp", lambda e: e.dma_start(out=d, in_=ap), "tap_" + name, reads=[trk])

    epoch = {}

    def sb(name, shape, dt=F32):
        cm = nc.sbuf_tensor("sb_" + name, shape, dt, align_bytes=64)
        t = cm.__enter__()
        stack.append(cm)
        b_ = Buf(t)
        b_.k.r = dict(epoch)
        return b_

    def new_epoch():
        for k_ in ("pe", "dve", "act"):
            epoch[k_] = S.cnt[k_]
        for k_, v_ in S.dma_cnt.items():
            epoch[k_] = v_

    banks = []
    for i in range(8):
        cm = nc.psum_tensor("ps%d" % i, [128, 512], F32)
        banks.append(Buf(cm.__enter__()))
        banks[-1].k.ps = True
        stack.append(cm)
    bank_i = [0]

    def PS():
        b = banks[bank_i[0] % 8]
        bank_i[0] += 1
        return b

    consts = sb("consts", [128, NCONST])
    nw = sb("nw", [128, 3, 8])
    nwrow = sb("nwrow", [128, D])
    ident_b = sb("identb", [128, 128], BF16)
    S.dma("sp", lambda e: e.dma_start(out=consts[:, :], in_=const_d[:, :]), "c0", writes=[consts.k])
    S.dma("sp", lambda e: e.dma_start(out=nw[:, :, :].rearrange("p a k -> p (a k)"), in_=nw_d[:, :]), "c1",
          writes=[nw.k])
    S.dma("sp", lambda e: e.dma_start(out=nwrow[:, :], in_=nwr_d[0:1, :].partition_broadcast(128)), "c2",
          writes=[nwrow.k])
    S.op("dve", lambda e: e.tensor_copy(ident_b[:, :], consts[:, C_ID:C_ID + 128]), reads=[consts.k],
         writes=[ident_b.k])
    cst = lambda o, n=128: consts[:, o:o + n]

    def norm_transpose(xt, xt_k, xn, sq, st, dstT, dst_k, col0, widx):
        S.op("act", lambda e: e.activation(sq[:, :], xt, AF.Square, accum_out=st[:, 0:1]),
             reads=[xt_k], writes=[sq.k, st.k])
        S.op("act", lambda e: e.activation(st[:, 1:2], st[:, 0:1], AF.Ln, bias=EPS, scale=1.0 / D),
             reads=[st.k], writes=[st.k])
        S.op("act", lambda e: e.activation(st[:, 2:3], st[:, 1:2], AF.Exp, scale=-0.5), reads=[st.k], writes=[st.k])
        S.op("dve", lambda e: e.tensor_scalar(xn[:, :], xt, st[:, 2:3], None, ALU.mult),
             reads=[st.k, xt_k], writes=[xn.k])
        p = PS()
        pb = p.t[:, :].bitcast(BF16)
        for k in range(8):
            S.op("pe", lambda e, k=k: e.transpose(pb[:, k * 128:(k + 1) * 128], xn[:, k * 128:(k + 1) * 128],
                                                  ident_b[:, :]),
                 reads=[xn.k, ident_b.k], writes=[p.k])
        S.op("dve", lambda e: e.tensor_tensor(
            dstT[:, :, col0:col0 + 128], pb.rearrange("p (k t) -> p k t", k=8),
            nw[:, widx, :].unsqueeze(2).broadcast_to([128, 8, 128]), ALU.mult),
            reads=[p.k, nw.k], writes=[dst_k])

    try:
        mark_a = len(stack)
        wcm = sb("wcm", [128, 8, NCM], BF16)
        wtm = sb("wtm", [128, 8, NTM], BF16)
        for k in range(8):
            S.dma("pool", lambda e, k=k: e.dma_start(out=wcm[:, k, :], in_=wcm_d[k * 128:(k + 1) * 128, :]),
                  "w%d" % (k % 4), writes=[wcm.k])
            S.dma("pool", lambda e, k=k: e.dma_start(out=wtm[:, k, :], in_=wtm_d[k * 128:(k + 1) * 128, :]),
                  "w%d" % (k % 4), writes=[wtm.k])
        for kk_ in range(4):
            wcm.k.w["d_w%d" % kk_] = S.dma_cnt["d_w%d" % kk_]
            wtm.k.w["d_w%d" % kk_] = S.dma_cnt["d_w%d" % kk_]
        cw = sb("cw", [128, 14, 5])
        S.dma("sp", lambda e: e.dma_start(out=cw[:, :, :].rearrange("p b k -> p (b k)"), in_=cw_d[:, :]), "c3",
              writes=[cw.k])
        rowp = sb("rowp", [128, 672])
        S.dma("sp", lambda e: e.dma_start(out=rowp[:, :], in_=rowp_d[0:1, :].partition_broadcast(128)), "c4",
              writes=[rowp.k])
        negA = sb("negA", [128, 12])
        S.op("act", lambda e: e.activation(negA[:, :], rowp[:, 12:24], AF.Exp), reads=[rowp.k], writes=[negA.k])
        S.op("dve", lambda e: e.tensor_scalar(negA[:, :], negA[:, :], -1.0, None, ALU.mult), reads=[negA.k],
             writes=[negA.k])

        xt2 = [sb("xt%d" % i, [128, D]) for i in range(2)]
        xn = sb("xn", [128, D], BF16)
        sq = sb("sq", [128, D], BF16)
        st = sb("st", [128, 4])
        xnT = sb("xnT", [128, 8, 512], BF16)
        halo = sb("halo", [128, 14, 3])
        pre = [sb("pre%d" % i, [128, 515]) for i in range(2)]
        acc = [sb("acc%d" % i, [128, 512]) for i in range(2)]
        cma = [sb("cma%d" % i, [128, 512]) for i in range(14)]
        rs = sb("rs", [128, 512])
        zs = [sb("zs0", [128, 512])] * 2
        zd = [sb("zd%d" % i, [128, 512]) for i in range(2)]
        hand = {}
        sm = [sb("sm%d" % i, [128, 16]) for i in range(2)]
        aall = [sb("aall%d" % i, [128, 12]) for i in range(2)]
        S.op("dve", lambda e: e.memset(halo[:, :, :], 0.0), writes=[halo.k])

        x_tok = sb("x_tok", [128, 512])
        xdt = sb("xdt", [128, 512], BF16)
        xdtd2 = [sb("xdtd%d" % i, [128, 512], BF16) for i in range(2)]
        b_tok = sb("b_tok", [128, 128], BF16)
        k_tok = sb("k_tok", [128, 256])
        vb = sb("vb", [128, 512], BF16)
        kbg = sb("kbg", [128, 512], BF16)
        kdec = sb("kdec", [128, 512], BF16)
        rhs1 = sb("rhs1", [128, 768])
        rhs2 = sb("rhs2", [128, 768])
        rhsb = sb("rhsb", [128, 256])
        LT = sb("LT", [128, 512])
        DT = sb("DT", [128, 256])
        Dn = sb("Dn", [128, 256])
        esm = sb("esm", [128, 24])
        cdc = [sb("cd%d" % i, [128, 12]) for i in range(2)]
        bg = sb("bg", [128, 4])
        scT = sb("scT", [128, 8, 128], BF16)
        t2 = sb("t2", [128, 256])
        t3 = sb("t3", [128, 256])
        Qm = [sb("Qm%d" % i, [128, 4, 128], BF16) for i in range(2)]
        Pm = [sb("Pm%d" % i, [128, 4, 128], BF16) for i in range(2)]
        Xm = [sb("Xm%d" % i, [128, 4, 128], BF16) for i in range(2)]
        qkT = sb("qkT", [128, 4, 128], BF16)
        u_sb = sb("u_sb", [128, 512])
        wT = sb("wT", [128, 4, 128])
        vnew2 = [sb("vnew%d" % i, [128, 512], BF16) for i in range(2)]
        o_sb = sb("o_sb", [128, 512])
        t1 = sb("t1", [128, 512])
        y1 = sb("y1", [128, 512])
        y2 = sb("y2", [128, 512])
        yb = sb("yb", [128, 1024], BF16)
        st2 = sb("st2", [128, 16])
        Sss = [sb("Sss%d" % i, [128, 512]) for i in range(2)]
        Sdn = [sb("Sdn%d" % i, [128, 512]) for i in range(2)]
        yT = [sb("yT0", [128, 8, 512], BF16)] * 2
        for b_ in (scT, qkT, Qm[0], Qm[1], Pm[0], Pm[1], Xm[0], Xm[1]):
            S.op("dve", lambda e, b_=b_: e.memset(b_[:, :, :], 0.0), writes=[b_.k])
        for b_ in (xdtd2[0], xdtd2[1], vnew2[0], vnew2[1]):
            S.op("dve", lambda e, b_=b_: e.memset(b_[:, :], 0.0), writes=[b_.k])
        S.op("dve", lambda e: e.memset(Sss[0][:, :], 0.0), writes=[Sss[0].k])
        S.op("dve", lambda e: e.memset(Sdn[0][:, :], 0.0), writes=[Sdn[0].k])
        s_par = [0]

        HV = lambda ap, h, n: ap.rearrange("p (h n) -> p h n", h=h)

        def phase_a_tile(ti, stl, col0):
            par = ti % 2
            tk = slice(col0, col0 + 128)
            for (dst, c0, w) in ((zs[par], 0, 512), (zd[par], 512, 512)):
                p = PS()
                for k in range(8):
                    S.op("pe", lambda e, k=k, p=p, c0=c0, w=w: e.matmul(p[:, 0:w], lhsT=xnT[:, k, tk],
                                                                        rhs=wtm[:, k, c0:c0 + w], start=(k == 0),
                                                                        stop=(k == 7)),
                         reads=[xnT.k, wtm.k], writes=[p.k])
                S.op("act", lambda e, p=p, dst=dst: e.activation(dst[:, :], p[:, 0:512], AF.Silu), reads=[p.k],
                     writes=[dst.k])
            ck('t_z')
            p = PS()
            for k in range(8):
                S.op("pe", lambda e, k=k, p=p: e.matmul(p[:, 0:16], lhsT=xnT[:, k, tk], rhs=wtm[:, k, 1024:1040],
                                                        start=(k == 0), stop=(k == 7)),
                     reads=[xnT.k, wtm.k], writes=[p.k])
            smt, aat = sm[par], aall[par]
            S.op("dve", lambda e: e.tensor_tensor(smt[:, 0:12], p[:, 0:12], rowp[:, 0:12], ALU.add),
                 reads=[p.k, rowp.k], writes=[smt.k])
            S.op("act", lambda e: e.activation(smt[:, 0:12], smt[:, 0:12], AF.Exp), reads=[smt.k], writes=[smt.k])
            S.op("act", lambda e: e.activation(smt[:, 0:12], smt[:, 0:12], AF.Ln, bias=1.0, scale=1.0),
                 reads=[smt.k], writes=[smt.k])
            S.op("act", lambda e: e.activation(smt[:, 12:16], p[:, 12:16], AF.Exp, scale=-1.0), reads=[p.k],
                 writes=[smt.k])
            S.op("dve", lambda e: e.tensor_scalar(smt[:, 12:16], smt[:, 12:16], 1.0, None, ALU.add), reads=[smt.k],
                 writes=[smt.k])
            S.op("dve", lambda e: e.reciprocal(smt[:, 12:16], smt[:, 12:16]), reads=[smt.k], writes=[smt.k])
            if ti == 0 and not os.environ.get('NOMS'):
                S.op("dve", lambda e: e.memset(smt[0:112, :], 0.0), reads=[smt.k], writes=[smt.k])
            S.op("dve", lambda e: e.tensor_tensor(aat[:, :], smt[:, 0:12], negA[:, :], ALU.mult),
                 reads=[smt.k, negA.k], writes=[aat.k])
            ck('t_small')
            px = PS()
            for i in range(4):
                S.op("pe", lambda e, i=i: e.transpose(px[:, i * 128:(i + 1) * 128], cma[i][:, tk], cst(C_ID)),
                     reads=[cma[i].k, consts.k], writes=[px.k])
            S.op("act", lambda e: e.copy(x_tok[:, :], px[:, :]), reads=[px.k], writes=[x_tok.k])
            S.op("dve", lambda e: e.tensor_tensor(HV(xdt[:, :], 8, 64), HV(px[:, :], 8, 64),
                                                  smt[:, 0:8].unsqueeze(2).broadcast_to([128, 8, 64]), ALU.mult),
                 reads=[px.k, smt.k] + ([x_tok.k] if os.environ.get('SER') else []), writes=[xdt.k])
            ck('t_px')
            pk = PS()
            for i, blk in enumerate((4, 8, 9)):
                S.op("pe", lambda e, i=i, blk=blk: e.transpose(pk[:, i * 128:(i + 1) * 128], cma[blk][:, tk],
                                                               cst(C_ID)),
                     reads=[cma[blk].k, consts.k], writes=[pk.k])
            S.op("act", lambda e: e.copy(b_tok[:, :], pk[:, 0:128]), reads=[pk.k], writes=[b_tok.k])
            S.op("act", lambda e: e.copy(k_tok[:, :], pk[:, 128:384]), reads=[pk.k], writes=[k_tok.k])
            ck('t_pk')
            pv = PS()
            for i in range(4):
                S.op("pe", lambda e, i=i: e.transpose(pv[:, i * 128:(i + 1) * 128], cma[10 + i][:, tk], cst(C_ID)),
                     reads=[cma[10 + i].k, consts.k], writes=[pv.k])
            S.op("dve", lambda e: e.tensor_tensor(HV(vb[:, :], 4, 128), HV(pv[:, :], 4, 128),
                                                  smt[:, 12:16].unsqueeze(2).broadcast_to([128, 4, 128]), ALU.mult),
                 reads=[pv.k, smt.k], writes=[vb.k])
            if ti == TAPT:
                tap('smt', smt[:, :], smt.k, [128, 16])
                tap('aat', aat[:, :], aat.k, [128, 12])
                tap('zs', zs[0][:, :], zs[0].k, [128, 512])
                tap('xdt', xdt[:, :], xdt.k, [128, 512])
                tap('vb', vb[:, :], vb.k, [128, 512])
                tap('k_tok', k_tok[:, :], k_tok.k, [128, 256])
            ck('t_tr')
            yield
            S.op("dve", lambda e: e.tensor_tensor(HV(rhs1[:, :], 12, 64),
                                                  aat[:, :].unsqueeze(2).broadcast_to([128, 12, 64]),
                                                  cst(C_U2, 64).unsqueeze(1).broadcast_to([128, 12, 64]), ALU.mult),
                 reads=[aat.k, consts.k], writes=[rhs1.k])
            S.op("dve", lambda e: e.tensor_copy(HV(rhs2[:, :], 12, 64),
                                                aat[:, :].unsqueeze(2).broadcast_to([128, 12, 64])),
                 reads=[aat.k], writes=[rhs2.k])
            pL = PS()
            S.op("pe", lambda e: e.matmul(pL[:, :], lhsT=cst(C_BD), rhs=rhs1[:, 0:512], start=True, stop=False),
                 reads=[rhs1.k, consts.k], writes=[pL.k])
            S.op("pe", lambda e: e.matmul(pL[:, :], lhsT=cst(C_NLE), rhs=rhs2[:, 0:512], start=False, stop=False),
                 reads=[rhs2.k], writes=[pL.k])
            S.op("pe", lambda e: e.matmul(pL[:, :], lhsT=cst(C_ID), rhs=cst(C_MA, 512), start=False, stop=True),
                 reads=[consts.k], writes=[pL.k])
            S.op("act", lambda e: e.activation(LT[:, :], pL[:, :], AF.Exp), reads=[pL.k], writes=[LT.k])
            pD = PS()
            S.op("pe", lambda e: e.matmul(pD[:, 0:256], lhsT=cst(C_BD), rhs=rhs1[:, 512:768], start=True, stop=False),
                 reads=[rhs1.k, consts.k], writes=[pD.k])
            S.op("pe", lambda e: e.matmul(pD[:, 0:256], lhsT=cst(C_NLE), rhs=rhs2[:, 512:768], start=False,
                                          stop=False), reads=[rhs2.k], writes=[pD.k])
            S.op("pe", lambda e: e.matmul(pD[:, 0:256], lhsT=cst(C_ID), rhs=cst(C_MA, 256), start=False, stop=True),
                 reads=[consts.k], writes=[pD.k])
            S.op("pe", lambda e: e.matmul(pD[:, 256:512], lhsT=cst(C_LE), rhs=rhs2[:, 512:768], start=True,
                                          stop=False), reads=[rhs2.k], writes=[pD.k])
            S.op("pe", lambda e: e.matmul(pD[:, 256:512], lhsT=cst(C_NBD), rhs=rhs1[:, 512:768], start=False,
                                          stop=False), reads=[rhs1.k], writes=[pD.k])
            S.op("pe", lambda e: e.matmul(pD[:, 256:512], lhsT=cst(C_ID), rhs=cst(C_MN, 256), start=False, stop=True),
                 reads=[consts.k], writes=[pD.k])
            S.op("act", lambda e: e.activation(DT[:, :], pD[:, 0:256], AF.Exp), reads=[pD.k], writes=[DT.k])
            S.op("act", lambda e: e.activation(Dn[:, :], pD[:, 256:512], AF.Exp), reads=[pD.k], writes=[Dn.k])
            ck('t_dec')
            pS = PS()
            S.op("pe", lambda e: e.matmul(pS[:, 0:12], lhsT=cst(C_LE), rhs=aat[:, :], start=True, stop=True),
                 reads=[aat.k, consts.k], writes=[pS.k])
            S.op("pe", lambda e: e.matmul(pS[:, 12:24], lhsT=cst(C_GT), rhs=aat[:, :], start=True, stop=True),
                 reads=[aat.k, consts.k], writes=[pS.k])
            S.op("act", lambda e: e.activation(esm[:, :], pS[:, 0:24], AF.Exp), reads=[pS.k], writes=[esm.k])
            if ti == TAPT:
                tap('LT', LT[:, :], LT.k, [128, 512])
                tap('DT', DT[:, :], DT.k, [128, 256])
                tap('Dn', Dn[:, :], Dn.k, [128, 256])
                tap('esm', esm[:, :], esm.k, [128, 24])
            ck('t_sm')
            pcb = PS()
            S.op("pe", lambda e: e.matmul(pcb[:, 0:128], lhsT=cma[4][:, tk], rhs=cma[5][:, tk], start=True, stop=True),
                 reads=[cma[4].k, cma[5].k], writes=[pcb.k])
            for hf in range(2):
                r = slice(64 * hf, 64 * hf + 64)
                S.op("dve", lambda e, r=r: e.tensor_tensor(
                    scT[r, :, r], HV(LT[r, :], 8, 64), pcb[r, r].unsqueeze(1).broadcast_to([64, 8, 64]), ALU.mult),
                    reads=[LT.k, pcb.k], writes=[scT.k])
            pY = PS()
            for j in range(8):
                S.op("pe", lambda e, j=j: e.matmul(pY[:, j * 64:(j + 1) * 64], lhsT=scT[:, j, :],
                                                   rhs=xdt[:, j * 64:(j + 1) * 64], start=True, stop=True),
                     reads=[scT.k, xdt.k], writes=[pY.k])
            hand[ti] = pY

        ones_f = sb("ones_f", [128, 128])
        S.op("dve", lambda e: e.memset(ones_f[:, :], 1.0), writes=[ones_f.k])

        def phase_a_tile2(ti, col0, u_idx, ucol):
            par = ti % 2
            pY = hand[ti]
            tk = slice(col0, col0 + 128)
            smt, aat = sm[par], aall[par]
            pC = PS()
            for c in range(2):
                r = slice(64 * c, 64 * c + 64)
                S.op("pe", lambda e, c=c, r=r: e.matmul(pC[:, 16 * c:16 * c + 12], lhsT=cst(C_C0 + 128 * c), rhs=aat[:, :],
                                                        start=True, stop=True),
                     reads=[aat.k, consts.k], writes=[pC.k])
            for c in range(2):
                S.op("act", lambda e, c=c: e.activation(cdc[c][:, :], pC[:, 16 * c:16 * c + 12], AF.Exp),
                     reads=[pC.k], writes=[cdc[c].k])
            ck('a_cd')
            for c in range(2):
                r = slice(64 * c, 64 * c + 64)
                S.op("dve", lambda e, c=c, r=r: e.tensor_tensor(
                    HV(xdtd2[c][r, :], 8, 64), HV(xdt[r, :], 8, 64),
                    esm[r, 12:20].unsqueeze(2).broadcast_to([64, 8, 64]), ALU.mult),
                    reads=[xdt.k, esm.k], writes=[xdtd2[c].k])
            k4 = k_tok[:, :].rearrange("p (e k) -> p e k", e=2).unsqueeze(2).broadcast_to([128, 2, 2, 128])
            S.op("dve", lambda e: e.tensor_tensor(
                kdec[:, :].rearrange("p (e h k) -> p e h k", e=2, h=2), k4,
                esm[:, 20:24].rearrange("p (e h) -> p e h", e=2).unsqueeze(3).broadcast_to([128, 2, 2, 128]), ALU.mult),
                reads=[k_tok.k, esm.k], writes=[kdec.k])
            S.op("dve", lambda e: e.tensor_tensor(bg[:, :], smt[:, 12:16], esm[:, 8:12], ALU.mult),
                 reads=[smt.k, esm.k], writes=[bg.k])
            S.op("dve", lambda e: e.tensor_tensor(
                kbg[:, :].rearrange("p (e h k) -> p e h k", e=2, h=2), k4,
                bg[:, :].rearrange("p (e h) -> p e h", e=2).unsqueeze(3).broadcast_to([128, 2, 2, 128]), ALU.mult),
                reads=[k_tok.k, bg.k], writes=[kbg.k])
            sp0 = s_par[0]
            pO = [PS(), PS()]
            for c in range(2):
                r = slice(64 * c, 64 * c + 64)
                Sin, Sout = Sss[(sp0 + c) % 2], Sss[(sp0 + c + 1) % 2]
                S.op("pe", lambda e, c=c, Sin=Sin: e.matmul(pO[c][:, :], lhsT=cma[5][:, tk], rhs=Sin[:, :], start=True,
                                                            stop=True), reads=[cma[5].k, Sin.k], writes=[pO[c].k])
                pst = PS()
                S.op("pe", lambda e, c=c, pst=pst: e.matmul(pst[:, :], lhsT=b_tok[:, :], rhs=xdtd2[c][:, :], start=True,
                                                            stop=True), reads=[b_tok.k, xdtd2[c].k], writes=[pst.k])
                S.op("dve", lambda e, c=c, Sin=Sin, Sout=Sout: e.tensor_tensor(
                    HV(Sout[:, :], 8, 64), HV(Sin[:, :], 8, 64),
                    cdc[c][:, 0:8].unsqueeze(2).broadcast_to([128, 8, 64]), ALU.mult),
                    reads=[Sin.k, cdc[c].k], writes=[Sout.k])
                S.op("dve", lambda e, Sout=Sout, pst=pst: e.tensor_tensor(Sout[:, :], Sout[:, :], pst[:, :], ALU.add),
                     reads=[pst.k, Sout.k], writes=[Sout.k])
                S.op("dve", lambda e, c=c, r=r: e.tensor_tensor(
                    HV(t1[r, :], 8, 64), HV(pO[c][r, :], 8, 64),
                    esm[r, 0:8].unsqueeze(2).broadcast_to([64, 8, 64]), ALU.mult),
                    reads=[pO[c].k, esm.k], writes=[t1.k])
            S.op("dve", lambda e: e.tensor_tensor(y1[:, :], t1[:, :], pY[:, :], ALU.add), reads=[t1.k, pY.k],
                 writes=[y1.k])
            S.op("dve", lambda e: e.tensor_tensor(HV(y2[:, :], 8, 64), HV(x_tok[:, :], 8, 64),
                                                  rowp[:, 24:32].unsqueeze(2).broadcast_to([128, 8, 64]), ALU.mult),
                 reads=[x_tok.k, rowp.k], writes=[y2.k])
            S.op("dve", lambda e: e.tensor_tensor(y1[:, :], y1[:, :], y2[:, :], ALU.add), reads=[y1.k, y2.k],
                 writes=[y1.k])
            S.op("dve", lambda e: e.tensor_tensor(y1[:, :], y1[:, :], zs[par][:, :], ALU.mult), reads=[y1.k, zs[par].k],
                 writes=[y1.k])
            S.op("dve", lambda e: e.scalar_tensor_tensor(y2[:, :], y1[:, :], 1.0, y1[:, :], ALU.mult, ALU.mult,
                                                         accum_out=st2[:, 0:1]), reads=[y1.k], writes=[y2.k, st2.k])
            S.op("act", lambda e: e.activation(st2[:, 1:2], st2[:, 0:1], AF.Ln, bias=EPS, scale=1.0 / 512),
                 reads=[st2.k], writes=[st2.k])
            S.op("act", lambda e: e.activation(st2[:, 2:3], st2[:, 1:2], AF.Exp, scale=-0.5), reads=[st2.k],
                 writes=[st2.k])
            S.op("dve", lambda e: e.scalar_tensor_tensor(yb[:, 0:512], y1[:, :], st2[:, 2:3], rowp[:, 32:544], ALU.mult,
                                                         ALU.mult), reads=[y1.k, st2.k, rowp.k], writes=[yb.k])
            if ti == TAPT:
                tap('ys', yb[:, 0:512], yb.k, [128, 512], BF16)
                tap('y1', y1[:, :], y1.k, [128, 512])
                tap('Sss', Sss[sp0][:, :], Sss[sp0].k, [128, 512])
            ck('a_ssd')
            rb_ = rhsb
            S.op("dve", lambda e: e.tensor_tensor(HV(rb_[:, :], 4, 64),
                                                  cst(C_I2, 64).unsqueeze(1).broadcast_to([128, 4, 64]),
                                                  smt[:, 12:16].unsqueeze(2).broadcast_to([128, 4, 64]), ALU.mult),
                 reads=[smt.k, consts.k], writes=[rb_.k])
            pB = PS()
            S.op("pe", lambda e: e.matmul(pB[:, 0:256], lhsT=cst(C_GT), rhs=rb_[:, :], start=True, stop=True),
                 reads=[rb_.k, consts.k], writes=[pB.k])
            pK = PS()
            for e_ in range(2):
                S.op("pe", lambda e, e_=e_: e.matmul(pK[:, e_ * 256:e_ * 256 + 128], lhsT=cma[8 + e_][:, tk],
                                                     rhs=cma[8 + e_][:, tk], start=True, stop=True),
                     reads=[cma[8 + e_].k], writes=[pK.k])
                S.op("pe", lambda e, e_=e_: e.matmul(pK[:, e_ * 256 + 128:e_ * 256 + 256], lhsT=cma[8 + e_][:, tk],
                                                     rhs=cma[6 + e_][:, tk], start=True, stop=True),
                     reads=[cma[8 + e_].k, cma[6 + e_].k], writes=[pK.k])
            S.op("dve", lambda e: e.tensor_tensor(t2[:, :], DT[:, :], pB[:, 0:256], ALU.mult), reads=[DT.k, pB.k],
                 writes=[t2.k])
            S.op("dve", lambda e: e.tensor_tensor(HV(t3[:, :], 4, 64), HV(Dn[:, :], 4, 64),
                                                  smt[:, 12:16].unsqueeze(2).broadcast_to([128, 4, 64]), ALU.mult),
                 reads=[Dn.k, smt.k], writes=[t3.k])
            pK4 = pK[:, :].rearrange("p (e w l) -> p e w l", e=2, w=2)
            for hf in range(2):
                r = slice(64 * hf, 64 * hf + 64)
                kkb = pK4[r, :, 0, r].unsqueeze(2).broadcast_to([64, 2, 2, 64])
                kqb = pK4[r, :, 1, r].unsqueeze(2).broadcast_to([64, 2, 2, 64])
                e4 = lambda ap: ap.rearrange("p (e h) l -> p e h l", e=2)
                f4 = lambda ap: ap.rearrange("p (e h l) -> p e h l", e=2, h=2)
                S.op("dve", lambda e, r=r, kkb=kkb: e.tensor_tensor(e4(Qm[0][r, :, r]), f4(t2[r, :]), kkb, ALU.mult),
                     reads=[t2.k, pK.k], writes=[Qm[0].k])
                S.op("dve", lambda e, r=r, kkb=kkb: e.tensor_tensor(e4(Pm[0][r, :, r]), f4(t3[r, :]), kkb, ALU.mult),
                     reads=[t3.k, pK.k], writes=[Pm[0].k])
                S.op("dve", lambda e, r=r, kqb=kqb: e.tensor_tensor(e4(qkT[r, :, r]), f4(DT[r, :]), kqb, ALU.mult),
                     reads=[DT.k, pK.k], writes=[qkT.k])
                S.op("dve", lambda e, r=r: e.scalar_tensor_tensor(
                    Xm[0][r, :, r], Qm[0][r, :, r], -1.0,
                    consts[r, C_I2:C_I2 + 64].unsqueeze(1).broadcast_to([64, 4, 64]), ALU.mult, ALU.add),
                    reads=[Qm[0].k, consts.k], writes=[Xm[0].k])
            if ti == TAPT:
                tap('Q0', Qm[0][:, :, :], Qm[0].k, [128, 4, 128])
                tap('P0', Pm[0][:, :, :], Pm[0].k, [128, 4, 128])
                tap('qkT', qkT[:, :, :], qkT.k, [128, 4, 128])
            ck('a_x0')
            cur = 0
            for k in range(6):
                nxt = 1 - cur
                Qc, Pc, Xc, Qn, Pn, Xn = Qm[cur], Pm[cur], Xm[cur], Qm[nxt], Pm[nxt], Xm[nxt]
                pq = PS() if k < 5 else None
                px_ = PS() if k >= 1 else None
                pp = PS() if k < 4 else None
                if pq is not None:
                    for h in range(4):
                        S.op("pe", lambda e, h=h, Pc=Pc, Qc=Qc, pq=pq: e.matmul(
                            pq[:, h * 128:(h + 1) * 128], lhsT=Pc[:, h, :], rhs=Qc[:, h, :], start=True, stop=True),
                            reads=[Pc.k, Qc.k], writes=[pq.k])
                if pp is not None:
                    for h in range(4):
                        S.op("pe", lambda e, h=h, Pc=Pc, Qc=Qc, pp=pp: e.matmul(
                            pp[:, h * 128:(h + 1) * 128], lhsT=Qc[:, h, :], rhs=Pc[:, h, :], start=True, stop=True),
                            reads=[Pc.k, Qc.k], writes=[pp.k])
                if px_ is not None:
                    for h in range(4):
                        S.op("pe", lambda e, h=h, Pc=Pc, Xc=Xc, px_=px_: e.matmul(
                            px_[:, h * 128:(h + 1) * 128], lhsT=Pc[:, h, :], rhs=Xc[:, h, :], start=True, stop=True),
                            reads=[Pc.k, Xc.k], writes=[px_.k])
                if pq is not None:
                    S.op("act", lambda e, Qn=Qn, pq=pq: e.copy(Qn[:, :, :], HV(pq[:, :], 4, 128)), reads=[pq.k],
                         writes=[Qn.k])
                if pp is not None:
                    S.op("act", lambda e, Pn=Pn, pp=pp: e.copy(Pn[:, :, :], HV(pp[:, :], 4, 128)), reads=[pp.k],
                         writes=[Pn.k])
                elif k == 4:
                    pass
                if px_ is not None:
                    S.op("dve", lambda e, Xn=Xn, Xc=Xc, px_=px_: e.tensor_tensor(
                        Xn[:, :, :], Xc[:, :, :], HV(px_[:, :], 4, 128), ALU.add), reads=[Xc.k, px_.k], writes=[Xn.k])
                else:
                    S.op("dve", lambda e, Xn=Xn, Xc=Xc: e.tensor_copy(Xn[:, :, :], Xc[:, :, :]), reads=[Xc.k],
                         writes=[Xn.k])
                if k == 4:
                    pp5 = PS()
                    for h in range(4):
                        S.op("pe", lambda e, h=h, Pc=Pc, Qc=Qc: e.matmul(
                            pp5[:, h * 128:(h + 1) * 128], lhsT=Qc[:, h, :], rhs=Pc[:, h, :], start=True, stop=True),
                            reads=[Pc.k, Qc.k], writes=[pp5.k])
                    S.op("act", lambda e, Pn=Pn: e.copy(Pn[:, :, :], HV(pp5[:, :], 4, 128)), reads=[pp5.k],
                         writes=[Pn.k])
                cur = nxt
            Xf = Xm[cur]
            ck('a_neu')
            pU, pW = PS(), PS()
            for h in range(4):
                S.op("pe", lambda e, h=h: e.matmul(pU[:, h * 128:(h + 1) * 128], lhsT=Xf[:, h, :],
                                                   rhs=vb[:, h * 128:(h + 1) * 128], start=True, stop=True),
                     reads=[Xf.k, vb.k], writes=[pU.k])
                S.op("pe", lambda e, h=h: e.matmul(pW[:, h * 128:(h + 1) * 128], lhsT=kbg[:, h * 128:(h + 1) * 128],
                                                   rhs=Xf[:, h, :], start=True, stop=True),
                     reads=[Xf.k, kbg.k], writes=[pW.k])
            S.op("act", lambda e: e.copy(u_sb[:, :], pU[:, :]), reads=[pU.k], writes=[u_sb.k])
            S.op("act", lambda e: e.copy(wT[:, :, :], HV(pW[:, :], 4, 128)), reads=[pW.k], writes=[wT.k])
            if ti == TAPT:
                tap('Xf', Xf[:, :, :], Xf.k, [128, 4, 128])
                tap('u', u_sb[:, :], u_sb.k, [128, 512])
                tap('wT', wT[:, :, :], wT.k, [128, 4, 128])
            ck('a_uw')
            pO1 = [PS(), PS()]
            for c in range(2):
                r = slice(64 * c, 64 * c + 64)
                Sin, Sout = Sdn[(sp0 + c) % 2], Sdn[(sp0 + c + 1) % 2]
                pws = PS()
                for h in range(4):
                    hs = slice(h * 128, (h + 1) * 128)
                    S.op("pe", lambda e, h=h, hs=hs, Sin=Sin, pws=pws: e.matmul(pws[:, hs], lhsT=wT[:, h, :],
                                                                                rhs=Sin[:, hs], start=True, stop=True),
                         reads=[wT.k, Sin.k], writes=[pws.k])
                for h in range(4):
                    hs = slice(h * 128, (h + 1) * 128)
                    S.op("pe", lambda e, h=h, hs=hs, Sin=Sin, c=c: e.matmul(pO1[c][:, hs], lhsT=cma[6 + h // 2][:, tk],
                                                                            rhs=Sin[:, hs], start=True, stop=True),
                         reads=[cma[6 + h // 2].k, Sin.k], writes=[pO1[c].k])
                S.op("dve", lambda e, r=r, c=c, pws=pws: e.tensor_tensor(vnew2[c][r, :], u_sb[r, :], pws[r, :], ALU.subtract),
                     reads=[u_sb.k, pws.k], writes=[vnew2[c].k])
                pSn = PS()
                for h in range(4):
                    hs = slice(h * 128, (h + 1) * 128)
                    S.op("pe", lambda e, hs=hs, c=c, pSn=pSn: e.matmul(pSn[:, hs], lhsT=kdec[:, hs], rhs=vnew2[c][:, hs],
                                                                       start=True, stop=True),
                         reads=[kdec.k, vnew2[c].k], writes=[pSn.k])
                for h in range(4):
                    hs = slice(h * 128, (h + 1) * 128)
                    S.op("dve", lambda e, hs=hs, h=h, c=c, Sin=Sin, Sout=Sout, pSn=pSn: e.scalar_tensor_tensor(
                        Sout[:, hs], Sin[:, hs], cdc[c][:, 8 + h:9 + h], pSn[:, hs], ALU.mult, ALU.add),
                        reads=[Sin.k, cdc[c].k, pSn.k], writes=[Sout.k])
                S.op("dve", lambda e, r=r, c=c: e.tensor_tensor(
                    HV(o_sb[r, :], 4, 128), HV(pO1[c][r, :], 4, 128),
                    esm[r, 8:12].unsqueeze(2).broadcast_to([64, 4, 128]), ALU.mult),
                    reads=[pO1[c].k, esm.k], writes=[o_sb.k])
            s_par[0] = sp0
            pO2 = PS()
            for h in range(4):
                hs = slice(h * 128, (h + 1) * 128)
                for c in range(2):
                    S.op("pe", lambda e, h=h, hs=hs, c=c: e.matmul(pO2[:, hs], lhsT=qkT[:, h, :], rhs=vnew2[c][:, hs],
                                                                   start=(c == 0), stop=(c == 1)),
                         reads=[qkT.k, vnew2[c].k], writes=[pO2.k])
            S.op("dve", lambda e: e.tensor_tensor(o_sb[:, :], o_sb[:, :], pO2[:, :], ALU.add), reads=[o_sb.k, pO2.k],
                 writes=[o_sb.k])
            if ti == TAPT:
                tap('vnew0', vnew2[0][:, :], vnew2[0].k, [128, 512])
                tap('vnew1', vnew2[1][:, :], vnew2[1].k, [128, 512])
                tap('o_pre', o_sb[:, :], o_sb.k, [128, 512])
                tap('Sdn', Sdn[sp0][:, :], Sdn[sp0].k, [128, 512])
            ck('a_rec')
            yield
            S.op("dve", lambda e: e.tensor_tensor(t1[:, :], o_sb[:, :], o_sb[:, :], ALU.mult), reads=[o_sb.k],
                 writes=[t1.k])
            S.op("dve", lambda e: e.tensor_reduce(st2[:, 4:8], HV(t1[:, :], 4, 128), AX.X, ALU.add), reads=[t1.k],
                 writes=[st2.k])
            S.op("act", lambda e: e.activation(st2[:, 8:12], st2[:, 4:8], AF.Ln, bias=EPS, scale=1.0 / 128),
                 reads=[st2.k], writes=[st2.k])
            S.op("act", lambda e: e.activation(st2[:, 12:16], st2[:, 8:12], AF.Exp, scale=-0.5), reads=[st2.k],
                 writes=[st2.k])
            S.op("dve", lambda e: e.tensor_tensor(HV(o_sb[:, :], 4, 128), HV(o_sb[:, :], 4, 128),
                                                  st2[:, 12:16].unsqueeze(2).broadcast_to([128, 4, 128]), ALU.mult),
                 reads=[o_sb.k, st2.k], writes=[o_sb.k])
            S.op("dve", lambda e: e.tensor_tensor(HV(o_sb[:, :], 4, 128), HV(o_sb[:, :], 4, 128),
                                                  rowp[:, 544:672].unsqueeze(1).broadcast_to([128, 4, 128]), ALU.mult),
                 reads=[o_sb.k, rowp.k], writes=[o_sb.k])
            S.op("dve", lambda e: e.tensor_tensor(yb[:, 512:1024], o_sb[:, :], zd[par][:, :], ALU.mult),
                 reads=[o_sb.k, zd[par].k], writes=[yb.k])
            if ti == TAPT:
                tap('yd', yb[:, 512:1024], yb.k, [128, 512], BF16)
            if u_idx is not None:
                pt = PS()
                ptb = pt.t[:, :].bitcast(BF16)
                for i in range(8):
                    S.op("pe", lambda e, i=i: e.transpose(ptb[:, i * 128:(i + 1) * 128], yb[:, i * 128:(i + 1) * 128],
                                                          ident_b[:, :]), reads=[yb.k, ident_b.k], writes=[pt.k])
                ys_ = yT[u_idx % 2]
                S.op("act", lambda e: e.copy(ys_[:, :, ucol:ucol + 128], ptb.rearrange("p (b t) -> p b t", b=8)),
                     reads=[pt.k], writes=[ys_.k])

        def phase_a_super(tiles):
            n = len(tiles)
            N = 128 * n
            for i, ti in enumerate(tiles):
                xt = xt2[ti % 2]
                if ti == 0 and not os.environ.get('FULL0'):
                    S.op("dve", lambda e: e.memset(xt[:, :], 0.0), writes=[xt.k])
                    S.dma("sp", lambda e: e.dma_start(out=xt[112:128, :], in_=xin[112:128, :]), "x%d" % (ti % 2),
                          writes=[xt.k])
                else:
                    S.dma("sp", lambda e, ti=ti, xt=xt: e.dma_start(out=xt[:, :], in_=xin[ti * 128:(ti + 1) * 128, :]),
                          "x%d" % (ti % 2), writes=[xt.k])
                norm_transpose(xt[:, :], xt.k, xn, sq, st, xnT, xnT.k, i * 128, 0)
            ck('a_norm')
            for blk in range(14):
                p = PS()
                for k in range(8):
                    S.op("pe", lambda e, k=k, p=p, blk=blk: e.matmul(p[:, 0:N], lhsT=wcm[:, k, blk * 128:(blk + 1) * 128],
                                                                     rhs=xnT[:, k, 0:N], start=(k == 0), stop=(k == 7)),
                         reads=[wcm.k, xnT.k], writes=[p.k])
                pr, ac = pre[blk % 2], acc[blk % 2]
                S.op("act", lambda e, pr=pr, blk=blk: e.copy(pr[:, 0:3], halo[:, blk, :]), reads=[halo.k], writes=[pr.k])
                S.op("act", lambda e, pr=pr, p=p: e.copy(pr[:, 3:3 + N], p[:, 0:N]), reads=[p.k], writes=[pr.k])
                S.op("act", lambda e, pr=pr, blk=blk: e.copy(halo[:, blk, :], pr[:, N:N + 3]), reads=[pr.k],
                     writes=[halo.k])
                S.op("dve", lambda e, pr=pr, ac=ac, blk=blk: e.tensor_scalar(ac[:, 0:N], pr[:, 0:N], cw[:, blk, 0:1], None,
                                                                             ALU.mult), reads=[pr.k, cw.k], writes=[ac.k])
                for j in range(1, 4):
                    S.op("dve", lambda e, pr=pr, ac=ac, blk=blk, j=j: e.scalar_tensor_tensor(
                        ac[:, 0:N], pr[:, j:j + N], cw[:, blk, j:j + 1], ac[:, 0:N], ALU.mult, ALU.add),
                        reads=[pr.k, cw.k, ac.k], writes=[ac.k])
                S.op("act", lambda e, ac=ac, blk=blk: e.activation(cma[blk][:, 0:N], ac[:, 0:N], AF.Silu,
                                                                   bias=cw[:, blk, 4:5], scale=1.0),
                     reads=[ac.k, cw.k], writes=[cma[blk].k])
            ck('a_cm')
            for blk in (6, 7, 8, 9):
                S.op("dve", lambda e, blk=blk: e.tensor_tensor(acc[0][:, 0:N], cma[blk][:, 0:N], cma[blk][:, 0:N], ALU.mult),
                     reads=[cma[blk].k], writes=[acc[0].k])
                p = PS()
                S.op("pe", lambda e, p=p: e.matmul(p[:, 0:N], lhsT=ones_f[:, :], rhs=acc[0][:, 0:N], start=True,
                                                   stop=True), reads=[ones_f.k, acc[0].k], writes=[p.k])
                S.op("act", lambda e, p=p: e.activation(rs[:, 0:N], p[:, 0:N], AF.Ln, bias=EPS, scale=1.0),
                     reads=[p.k], writes=[rs.k])
                S.op("act", lambda e: e.activation(rs[:, 0:N], rs[:, 0:N], AF.Exp, scale=-0.5), reads=[rs.k],
                     writes=[rs.k])
                sc = (128.0 ** -0.5) if blk < 8 else 1.0
                S.op("dve", lambda e, blk=blk, sc=sc: e.scalar_tensor_tensor(
                    cma[blk][:, 0:N], cma[blk][:, 0:N], sc, rs[:, 0:N], ALU.mult, ALU.mult),
                    reads=[cma[blk].k, rs.k], writes=[cma[blk].k])
            if tiles[0] == TAPT:
                tap('xnT', xnT[:, :, 0:128], xnT.k, [128, 8, 128], BF16)
                for b_ in range(14):
                    tap('cma%d' % b_, cma[b_][:, 0:128], cma[b_].k, [128, 128])
            ck('a_l2')
            heads = [phase_a_tile(ti, None, i * 128) for i, ti in enumerate(tiles)]
            next(heads[0])
            for i, ti in enumerate(tiles):
                for _ in heads[i]:
                    pass
                ck('a_t1')
                if ti == 0:
                    u_idx, ucol = None, 0
                else:
                    u_idx, ucol = (ti - 1) // 4, ((ti - 1) % 4) * 128
                g2 = phase_a_tile2(ti, i * 128, u_idx, ucol)
                next(g2)
                if i + 1 < n:
                    next(heads[i + 1])
                for _ in g2:
                    pass
                if ti >= 1 and (ti - 1) % 4 == 3:
                    u = (ti - 1) // 4
                    ys_ = yT[u % 2]
                    srcap = ysrc[u].ap()
                    dstap = yround[u // 4].ap()[(u % 4) * 4096:(u % 4 + 1) * 4096, :]
                    utk = Trk()
                    S.dma("sp", lambda e, ys_=ys_, srcap=srcap: e.dma_start(
                        out=srcap.rearrange("(b p) t -> p b t", p=128), in_=ys_[:, :, :]), "ys",
                        reads=[ys_.k], writes=[utk])
                    S.dma("pool", lambda e, srcap=srcap, dstap=dstap: e.collective_compute(
                        "AllGather", ALU.bypass, replica_groups=GROUPS, ins=[srcap.opt()], outs=[dstap.opt()]),
                        "ag", reads=[utk], writes=[ag_trk[u]], inc=1)

        ag_trk = [Trk() for _ in range(NU)]
        ck('setup')
        if not os.environ.get('SKIP0'):
            phase_a_super([0])
        ck('meta')
        if stop == 'meta#2':
            phase_a_super([0])
            ck('meta')
        for s_ in range(NU):
            phase_a_super([1 + 4 * s_ + j_ for j_ in range(4)])
            ck('super%d' % s_)

        ck('phaseA')
        while len(stack) > mark_a:
            stack.pop().__exit__(None, None, None)
        new_epoch()

        qv = {}

        def Q(e):
            if "q" not in qv:
                qv["q"] = e.partition_id() % 4
            return qv["q"]

        def XW(e):
            if "xw" not in qv:
                qv["xw"] = xin[bass.ds(Q(e) * 512, TOKA - 1536), :]
            return qv["xw"]

        def YW(e, bi):
            if ("yw", bi) not in qv:
                if "q4" not in qv:
                    qv["q4"] = Q(e) * 4096
                qv[("yw", bi)] = yround[bi].ap()[bass.ds(qv["q4"], 4096), :]
            return qv[("yw", bi)]

        NT_B = 4 * NBLK
        h1 = sb("h1", [128, NT_B, D])
        h1_k = [Trk() for _ in range(NT_B)]
        xn2T = sb("xn2T", [128, 8, 512 * NBLK], BF16)
        xn2_k = [Trk() for _ in range(NBLK)]
        xnb = sb("xnb", [128, D], BF16)
        sqb = sb("sqb", [128, D], BF16)
        stb = sb("stb", [128, 4])
        mark_mix = len(stack)
        wout = sb("wout", [128, 8, D], BF16)
        def load_wout():
            for k in range(8):
                S.dma("pool", lambda e, k=k: e.dma_start(out=wout[:, k, :], in_=wout_d[k * 128:(k + 1) * 128, :]),
                      "wo%d" % (k % 2), writes=[wout.k])
            for kk_ in range(2):
                wout.k.w["d_wo%d" % kk_] = S.dma_cnt["d_wo%d" % kk_]
        wgmP = [sb("wgm%d" % i, [128, 8, 2, 128], BF16) for i in range(2)]
        wbrmP = [sb("wbrm%d" % i, [128, 32, 128], BF16) for i in range(2)]
        xT = sb("xTb", [128, 8, 512], BF16)
        yTb = sb("yTb", [128, 32, 512], BF16)
        g_sb = [sb("g%d" % i, [128, 512]) for i in range(2)]
        mt = sb("mt", [128, 512])
        mrgT = sb("mrgT", [128, 8, 512], BF16)

        for bi in range(NBLK):
            row0 = 128 + 2048 * bi
            for i in range(4):
                t = 4 * bi + i
                S.dma("sp", lambda e, t=t, i=i, row0=row0: e.dma_start(
                    out=h1[:, t, :], in_=XW(e)[row0 + i * 128:row0 + (i + 1) * 128, :]), "xb%d" % (i % 2),
                    writes=[h1_k[t]])
                norm_transpose(h1[:, t, :], h1_k[t], xnb, sqb, stb, xT, xT.k, i * 128, 0)
            yr = yround[bi].ap()
            for c4 in range(4):
                S.dma("sp", lambda e, c4=c4, bi=bi: e.dma_start(
                        out=yTb[:, c4 * 8:(c4 + 1) * 8, :],
                        in_=YW(e, bi)[c4 * 1024:(c4 + 1) * 1024, :].rearrange("(c p) t -> p c t", p=128)),
                        "yl%d" % c4, reads=[ag_trk[4 * bi + j] for j in range(4)], writes=[yTb.k])
            if bi == 0:
                tap('b_xT', xT[:, :, :], xT.k, [128, 8, 512], BF16)
                tap('b_yT', yTb[:, :, :], yTb.k, [128, 32, 512], BF16)
            def load_wm(m):
                wgm_, wbrm_ = wgmP[m % 2], wbrmP[m % 2]
                S.dma("pool", lambda e, m=m, wgm_=wgm_: e.dma_start(out=wgm_[:, :, :, :].rearrange("p k n j -> p (k n j)"),
                                                                    in_=wgm_d[m, :, :]), "wm0_%d" % (m % 2), writes=[wgm_.k])
                S.dma("pool", lambda e, m=m, wbrm_=wbrm_: e.dma_start(out=wbrm_[:, :, :].rearrange("p c j -> p (c j)"),
                                                                      in_=wbrm_d[m, :, :]), "wm1_%d" % (m % 2), writes=[wbrm_.k])
            load_wm(0)
            if bi == 0:
                load_wm(1)
                load_wout()
            for m in range(8):
                if m + 1 < 8 and not (bi == 0 and m == 0):
                    load_wm(m + 1)
                wgm, wbrm = wgmP[m % 2], wbrmP[m % 2]
                pbs = []
                for n in range(2):
                    pg = PS()
                    for k in range(8):
                        S.op("pe", lambda e, k=k, n=n, pg=pg, wgm=wgm: e.matmul(pg[:, :], lhsT=wgm[:, k, n, :], rhs=xT[:, k, :],
                                                                       start=(k == 0), stop=(k == 7)),
                             reads=[wgm.k, xT.k], writes=[pg.k])
                    S.op("act", lambda e, n=n, pg=pg: e.activation(g_sb[n][:, :], pg[:, :], AF.Sigmoid), reads=[pg.k],
                         writes=[g_sb[n].k])
                    pb = PS()
                    for kk in range(16):
                        cidx = (kk // 4) * 8 + n * 4 + (kk % 4)
                        S.op("pe", lambda e, kk=kk, n=n, pb=pb, cidx=cidx, wbrm=wbrm: e.matmul(
                            pb[:, :], lhsT=wbrm[:, n * 16 + kk, :], rhs=yTb[:, cidx, :], start=(kk == 0),
                            stop=(kk == 15)), reads=[wbrm.k, yTb.k], writes=[pb.k])
                    pbs.append(pb)
                if bi == 0 and m == 0:
                    tap('b_g0', g_sb[0][:, :], g_sb[0].k, [128, 512])
                    tap('b_g1', g_sb[1][:, :], g_sb[1].k, [128, 512])
                S.op("dve", lambda e, pb=pbs[0]: e.tensor_tensor(mt[:, :], g_sb[0][:, :], pb[:, :], ALU.mult),
                     reads=[g_sb[0].k, pbs[0].k], writes=[mt.k])
                S.op("dve", lambda e, pb=pbs[1]: e.tensor_tensor(g_sb[1][:, :], g_sb[1][:, :], pb[:, :], ALU.mult),
                     reads=[g_sb[1].k, pbs[1].k], writes=[g_sb[1].k])
                S.op("dve", lambda e, m=m: e.tensor_tensor(mrgT[:, m, :], mt[:, :], g_sb[1][:, :], ALU.add),
                     reads=[mt.k, g_sb[1].k], writes=[mrgT.k])
            if bi == 0:
                tap('b_mrg', mrgT[:, :, :], mrgT.k, [128, 8, 512], BF16)
            for i in range(4):
                t = 4 * bi + i
                for hf in range(2):
                    p = PS()
                    for k in range(8):
                        S.op("pe", lambda e, k=k, i=i, hf=hf, p=p: e.matmul(
                            p[:, :], lhsT=mrgT[:, k, i * 128:(i + 1) * 128], rhs=wout[:, k, hf * 512:(hf + 1) * 512],
                            start=(k == 0), stop=(k == 7)), reads=[mrgT.k, wout.k], writes=[p.k])
                    S.op("dve", lambda e, t=t, hf=hf, p=p: e.tensor_tensor(
                        h1[:, t, hf * 512:(hf + 1) * 512], h1[:, t, hf * 512:(hf + 1) * 512], p[:, :], ALU.add),
                        reads=[p.k, h1_k[t]], writes=[h1_k[t]])
                if t == 0:
                    tap('b_h1', h1[:, 0, :], h1_k[0], [128, 1024])
                norm_transpose(h1[:, t, :], h1_k[t], xnb, sqb, stb, xn2T, xn2_k[bi], bi * 512 + i * 128, 1)

        while len(stack) > mark_mix:
            stack.pop().__exit__(None, None, None)
        new_epoch()

        ck('mix')
        wguq_fP = [sb("wguq%d" % i, [128, 8 * 2 * 6 * 128], BF16) for i in range(2)]
        wdnq_fP = [sb("wdnq%d" % i, [128, 6 * D], BF16) for i in range(2)]
        actT = sb("actT", [128, 6, 512], BF16)
        gs = sb("gs", [128, 512])
        ob = sb("ob", [128, D])

        def load_q(qi):
            b0, nb = QB[qi]
            w = nb * 128
            wg_, wd_ = wguq_fP[qi % 2], wdnq_fP[qi % 2]
            k0, k1 = "fq%d_0" % (qi % 2), "fq%d_1" % (qi % 2)
            for kn in range(16):
                S.dma("pool", lambda e, qi=qi, w=w, kn=kn, wg_=wg_: e.dma_start(
                    out=wg_[:, kn * w:(kn + 1) * w], in_=wguq_d[qi][:, kn * w:(kn + 1) * w]), (k0, k1)[kn % 2],
                    writes=[wg_.k])
            for c_ in range(nb):
                S.dma("pool", lambda e, qi=qi, c_=c_, wd_=wd_: e.dma_start(
                    out=wd_[:, c_ * D:(c_ + 1) * D], in_=wdnq_d[qi][:, c_ * D:(c_ + 1) * D]), (k0, k1)[c_ % 2],
                    writes=[wd_.k])
            for kk_ in (k0, k1):
                wg_.k.w["d_" + kk_] = S.dma_cnt["d_" + kk_]
                wd_.k.w["d_" + kk_] = S.dma_cnt["d_" + kk_]

        load_q(0)
        for qi, (b0, nb) in enumerate(QB):
            w = nb * 128
            if qi + 1 < 4:
                load_q(qi + 1)
            wguq_f, wdnq_f = wguq_fP[qi % 2], wdnq_fP[qi % 2]
            wguq = wguq_f[:, 0:16 * w].rearrange("p (k n j) -> p k n j", k=8, n=2)
            wdnq = wdnq_f[:, 0:nb * D].rearrange("p (c d) -> p c d", c=nb)
            for bi in range(NBLK):
                tk = slice(bi * 512, (bi + 1) * 512)
                for j in range(nb):
                    pg, pu = PS(), PS()
                    for (pp, n) in ((pg, 0), (pu, 1)):
                        for k in range(8):
                            S.op("pe", lambda e, k=k, n=n, j=j, pp=pp, tk=tk, wguq=wguq: e.matmul(
                                pp[:, :], lhsT=wguq[:, k, n, j * 128:(j + 1) * 128], rhs=xn2T[:, k, tk], start=(k == 0),
                                stop=(k == 7)), reads=[wguq_f.k, xn2_k[bi]], writes=[pp.k])
                    S.op("act", lambda e, pg=pg: e.activation(gs[:, :], pg[:, :], AF.Silu), reads=[pg.k], writes=[gs.k])
                    S.op("dve", lambda e, j=j, pu=pu: e.tensor_tensor(actT[:, j, :], gs[:, :], pu[:, :], ALU.mult),
                         reads=[gs.k, pu.k], writes=[actT.k])
                for i in range(4):
                    t = 4 * bi + i
                    for hf in range(2):
                        p = PS()
                        for j in range(nb):
                            S.op("pe", lambda e, j=j, i=i, hf=hf, p=p, nb=nb, wdnq=wdnq: e.matmul(
                                p[:, :], lhsT=actT[:, j, i * 128:(i + 1) * 128], rhs=wdnq[:, j, hf * 512:(hf + 1) * 512],
                                start=(j == 0), stop=(j == nb - 1)), reads=[actT.k, wdnq_f.k], writes=[p.k])
                        S.op("dve", lambda e, t=t, hf=hf, p=p: e.tensor_tensor(
                            h1[:, t, hf * 512:(hf + 1) * 512], h1[:, t, hf * 512:(hf + 1) * 512], p[:, :], ALU.add),
                            reads=[p.k, h1_k[t]], writes=[h1_k[t]])
        tap('b_h2', h1[:, 0, :], h1_k[0], [128, 1024])
        outs = []
        for t in range(NT_B):
            S.op("act", lambda e, t=t: e.activation(sqb[:, :], h1[:, t, :], AF.Square, accum_out=stb[:, 0:1]),
                 reads=[h1_k[t]], writes=[sqb.k, stb.k])
            S.op("act", lambda e: e.activation(stb[:, 1:2], stb[:, 0:1], AF.Sqrt, bias=EPS, scale=1.0 / D),
                 reads=[stb.k], writes=[stb.k])
            S.op("dve", lambda e: e.reciprocal(stb[:, 2:3], stb[:, 1:2]), reads=[stb.k], writes=[stb.k])
            S.op("dve", lambda e, t=t: e.scalar_tensor_tensor(ob[:, :], h1[:, t, :], stb[:, 2:3], nwrow[:, :], ALU.mult,
                                                              ALU.mult), reads=[h1_k[t], stb.k, nwrow.k], writes=[ob.k])
            outs.append(S.dma("sp", lambda e, t=t: e.dma_start(out=out_d[t * 128:(t + 1) * 128, :], in_=ob[:, :]),
                              "o%d" % (t % 2), reads=[ob.k]))
        S.final_wait("sp", outs[-2:] if len(outs) >= 2 else outs)
    except _Stop:
        pass
    if taps:
        S.final_wait("sp", list(taps.values()))
    build.last_sched = S
    S.emit_all()
    while stack:
        stack.pop().__exit__(None, None, None)
    S.close()
    return nc


_CACHE = {}


def _prep_core(b, g, x, meta_tokens, p):
    w_in = p["w_in"][0]
    sl = lambda o, n: w_in[:, o:o + n]
    wcm = np.concatenate([sl(2048 + 512 * g, 512), sl(4096 + 128 * g, 128), sl(4608 + 128 * g, 128),
                          sl(5152 + 256 * g, 256), sl(6176 + 256 * g, 256), sl(7200 + 512 * g, 512)], axis=1)
    wtm = np.concatenate([sl(512 * g, 512), sl(9280 + 512 * g, 512), sl(5120 + 8 * g, 8), sl(9248 + 4 * g, 4),
                          sl(9264 + 4 * g, 4)], axis=1)
    scw, scb, dcw = p["ssd_conv_w"][0], p["ssd_conv_b"][0], p["dn_conv_w"][0]
    cwl, cbl = [], []
    for (src, o, n, bias) in ((scw, 512 * g, 512, scb), (scw, 2048 + 128 * g, 128, scb),
                              (scw, 2560 + 128 * g, 128, scb), (dcw, 256 * g, 256, None),
                              (dcw, 1024 + 256 * g, 256, None), (dcw, 2048 + 512 * g, 512, None)):
        cwl.append(src[:, o:o + n].T)
        cbl.append(bias[o:o + n] if bias is not None else np.zeros(n, np.float32))
    cw = np.concatenate([np.concatenate(cwl, 0), np.concatenate(cbl)[:, None]], axis=1)
    cw = cw.reshape(14, 128, 5).transpose(1, 0, 2).reshape(128, 70)
    rowp = np.concatenate([p["ssd_dt_bias"][0][8 * g:8 * g + 8], p["dn_dt_bias"][0][4 * g:4 * g + 4],
                           p["ssd_a_log"][0][8 * g:8 * g + 8], p["dn_a_log"][0][4 * g:4 * g + 4],
                           p["ssd_d"][0][8 * g:8 * g + 8], p["ssd_norm_w"][0][512 * g:512 * g + 512],
                           p["dn_norm_w"][0]])[None, :]
    xin = np.concatenate([np.zeros((112, D), np.float32), meta_tokens, x[b]], axis=0)
    return {"xin": np.ascontiguousarray(xin, np.float32), "wcm": np.ascontiguousarray(wcm, np.float32),
            "wtm": np.ascontiguousarray(wtm, np.float32), "cw": np.ascontiguousarray(cw, np.float32),
            "rowp": np.ascontiguousarray(rowp, np.float32)}


def kernel(x, meta_tokens, mix_norm_w, w_in, ssd_conv_w, ssd_conv_b, ssd_dt_bias, ssd_a_log, ssd_d, ssd_norm_w,
           dn_conv_w, dn_dt_bias, dn_a_log, dn_norm_w, w_branch, w_out, ffn_norm_w, w_gate_up, w_down,
           final_norm_w):
    p = dict(w_in=w_in, ssd_conv_w=ssd_conv_w, ssd_conv_b=ssd_conv_b, ssd_dt_bias=ssd_dt_bias, ssd_a_log=ssd_a_log,
             ssd_d=ssd_d, ssd_norm_w=ssd_norm_w, dn_conv_w=dn_conv_w, dn_dt_bias=dn_dt_bias, dn_a_log=dn_a_log,
             dn_norm_w=dn_norm_w)
    p = {k: np.asarray(v, np.float32) for k, v in p.items()}
    x = np.asarray(x, np.float32)
    meta_tokens = np.asarray(meta_tokens, np.float32)
    B, L, _ = x.shape
    assert B == 2 and L % 2048 == 0
    NU = L // 512
    NBLK = NU // 4
    if NU not in _CACHE:
        _CACHE[NU] = build(NU)
    nc = _CACHE[NU]
    w_in0 = p["w_in"][0]
    wg = w_in0[:, 11328:13376].reshape(8, 128, 2, 8, 128)
    wgm = np.ascontiguousarray(wg.transpose(3, 1, 0, 2, 4)).reshape(8, 128, 8 * 2 * 128)
    wb = np.asarray(w_branch, np.float32)[0].reshape(32, 128, 8, 128)
    wbrm = np.ascontiguousarray(wb.transpose(2, 1, 0, 3)).reshape(8, 128, 32 * 128)
    wgu = np.asarray(w_gate_up, np.float32)[0].reshape(8, 128, 2, 22, 128)
    wdn = np.asarray(w_down, np.float32)[0].reshape(22, 128, D)
    nw3 = np.stack([np.asarray(mix_norm_w, np.float32)[0], np.asarray(ffn_norm_w, np.float32)[0],
                    np.asarray(final_norm_w, np.float32)])
    shared = {"nw": np.ascontiguousarray(nw3.reshape(3, 8, 128).transpose(2, 0, 1)).reshape(128, 24),
              "nwr": np.ascontiguousarray(np.asarray(final_norm_w, np.float32)[None, :]),
              "consts": make_consts(), "wgm": wgm, "wbrm": wbrm,
              "wout": np.ascontiguousarray(np.asarray(w_out, np.float32)[0])}
    for qi, (b0, nb) in enumerate([(0, 6), (6, 6), (12, 5), (17, 5)]):
        shared["wguq%d" % qi] = np.ascontiguousarray(
            wgu[:, :, :, b0:b0 + nb, :].transpose(1, 0, 2, 3, 4)).reshape(128, 8 * 2 * nb * 128)
        shared["wdnq%d" % qi] = np.ascontiguousarray(wdn[b0:b0 + nb].transpose(1, 0, 2)).reshape(128, nb * D)
    in_maps = []
    for c in range(8):
        m = _prep_core(c // 4, c % 4, x, meta_tokens, p)
        m.update(shared)
        in_maps.append(m)
    res = run_bass_kernel_spmd(nc, in_maps, core_ids=list(range(8)))
    kernel.last_res = res
    out = np.zeros((B, L, D), np.float32)
    for c in range(8):
        b, q = c // 4, c % 4
        o = np.asarray(res.results[c]["out"]).reshape(NBLK, 512, D)
        for bi in range(NBLK):
            u = 4 * bi + q
            out[b, 512 * u:512 * (u + 1)] = o[bi]
    return out
```
